# Optimizing a Trainium2 kernel written in Bass

```python
import math
import jax, jax.numpy as jnp
from jax import lax
import numpy as np

D_MODEL = 1024
BATCH = 8
SEQ = 4096
DEPTH = 1

CHUNK = 64
Q_BLOCK = 128
EPS = 1e-6
DA_HEADS = 8
DA_HEAD_DIM = 64
DA_WIDTH = DA_HEADS * 2 * DA_HEAD_DIM
ROPE_THETA = 10000.0
ML_HEADS = 4
ML_WIDTH = D_MODEL
ML_HEAD_DIM = ML_WIDTH // ML_HEADS
CONV_WIDTH = 4
D_FF = 2816
FFN_RES = 0.5
N_MOD = 9

kernel_name = "hybrid_diffattn_mlstm_macaron_adaln"


def rmsnorm(x, g):
    x32 = x.astype(jnp.float32)
    y = x32 * lax.rsqrt(jnp.mean(x32 * x32, axis=-1, keepdims=True) + EPS)
    return (y * g.astype(jnp.float32)).astype(x.dtype)


def modulate(xn, shift, scale):
    return xn * (1.0 + scale[:, None, :]) + shift[:, None, :]


def swiglu(u, w_gate, w_up, w_down):
    return (jax.nn.silu(u @ w_gate) * (u @ w_up)) @ w_down


def rope(t):
    S, Dh = t.shape[1], t.shape[-1]
    inv = ROPE_THETA ** (-jnp.arange(0, Dh, 2, dtype=jnp.float32) / Dh)
    ang = jnp.arange(S, dtype=jnp.float32)[:, None] * inv[None, :]
    cos = jnp.cos(ang)[None, :, None, None, :].astype(t.dtype)
    sin = jnp.sin(ang)[None, :, None, None, :].astype(t.dtype)
    t1, t2 = jnp.split(t, 2, axis=-1)
    return jnp.concatenate([t1 * cos - t2 * sin, t1 * sin + t2 * cos], axis=-1)


def diff_attention(q, k, v, lam, g_subln, lambda_init):
    B, S, H, _, Dh = q.shape
    scale = Dh ** -0.5
    q = rope(q)
    k = rope(k)
    nb = S // Q_BLOCK
    qb = jnp.moveaxis(q.reshape(B, nb, Q_BLOCK, H, 2, Dh), 1, 0)
    key_chunk = jnp.arange(S) // CHUNK

    def block(args):
        qi, bi = args
        s = jnp.einsum('bqhcd,bkhcd->bchqk', qi, k,
                       preferred_element_type=jnp.float32) * scale
        q_chunk = (bi * Q_BLOCK + jnp.arange(Q_BLOCK)) // CHUNK
        mask = key_chunk[None, :] <= q_chunk[:, None]
        s = jnp.where(mask, s, -jnp.inf)
        p = jax.nn.softmax(s, axis=-1)
        a = p[:, 0] - lam * p[:, 1]
        return jnp.einsum('bhqk,bkhd->bqhd', a.astype(v.dtype), v)

    o = lax.map(block, (qb, jnp.arange(nb)))
    o = jnp.moveaxis(o, 0, 1).reshape(B, S, H, 2 * Dh)
    o = rmsnorm(o, g_subln) * (1.0 - lambda_init)
    return o.reshape(B, S, H * 2 * Dh)


def causal_conv(x, w, b):
    K, C = w.shape
    y = lax.conv_general_dilated(x, w[:, None, :].astype(x.dtype), window_strides=(1,),
                                 padding=[(K - 1, 0)],
                                 dimension_numbers=('NWC', 'WIO', 'NWC'),
                                 feature_group_count=C)
    return y + b


def mlstm_chunkwise(q, k, v, i_pre, f_pre):
    B, H, S, D = q.shape
    L = CHUNK
    nc = S // L
    out_dtype = v.dtype
    q = q.astype(jnp.float32)
    k = k.astype(jnp.float32) * (D ** -0.5)
    v = v.astype(jnp.float32)
    ig = i_pre.astype(jnp.float32)
    logf = jax.nn.log_sigmoid(f_pre.astype(jnp.float32))

    def to_chunks(t):
        return jnp.moveaxis(t.reshape(B, H, nc, L, *t.shape[3:]), 2, 0)

    tril = jnp.tril(jnp.ones((L, L), dtype=bool))

    def body(carry, inp):
        C, n, m = carry
        qc, kc, vc, ic, fc = inp
        b = jnp.cumsum(fc, axis=-1)
        dmat = jnp.where(tril, b[..., :, None] - b[..., None, :] + ic[..., None, :], -jnp.inf)
        inter = b + m[..., None]
        m_t = jnp.maximum(jnp.max(dmat, axis=-1), inter)
        w = jnp.exp(dmat - m_t[..., None])
        sc = jnp.einsum('bhtd,bhsd->bhts', qc, kc) * w
        w_inter = jnp.exp(inter - m_t)
        num = jnp.einsum('bhts,bhsd->bhtd', sc, vc) + w_inter[..., None] * jnp.einsum('bhtd,bhde->bhte', qc, C)
        den = jnp.sum(sc, axis=-1) + w_inter * jnp.einsum('bhtd,bhd->bht', qc, n)
        h = num / jnp.maximum(jnp.abs(den), jnp.exp(-m_t))[..., None]
        bL = b[..., -1]
        g = bL[..., None] - b + ic
        m_new = jnp.maximum(bL + m, jnp.max(g, axis=-1))
        decay = jnp.exp(bL + m - m_new)
        wk = jnp.exp(g - m_new[..., None])[..., None] * kc
        C_new = decay[..., None, None] * C + jnp.einsum('bhsd,bhse->bhde', wk, vc)
        n_new = decay[..., None] * n + jnp.sum(wk, axis=-2)
        return (C_new, n_new, m_new), h

    init = (jnp.zeros((B, H, D, D), jnp.float32), jnp.zeros((B, H, D), jnp.float32),
            jnp.zeros((B, H), jnp.float32))
    _, hs = lax.scan(body, init, (to_chunks(q), to_chunks(k), to_chunks(v),
                                  to_chunks(ig), to_chunks(logf)))
    hs = jnp.moveaxis(hs, 0, 2).reshape(B, H, S, D)
    return hs.astype(out_dtype)


def mlstm_branch(xm, o_pre, conv_w, conv_b, w_mq, w_mk, w_mv, w_if, b_if, ml_skip, g_mlnorm):
    B, S, _ = xm.shape
    H, Dh = ML_HEADS, ML_HEAD_DIM
    xc = jax.nn.silu(causal_conv(xm, conv_w, conv_b))
    xch = xc.reshape(B, S, H, Dh)
    xmh = xm.reshape(B, S, H, Dh)
    q = jnp.einsum('bshd,hde->bhse', xch, w_mq)
    k = jnp.einsum('bshd,hde->bhse', xch, w_mk)
    v = jnp.einsum('bshd,hde->bhse', xmh, w_mv)
    gates = (jnp.einsum('bhse,heg->bgs', q, w_if[0]) + jnp.einsum('bhse,heg->bgs', k, w_if[1])
             + jnp.einsum('bhse,heg->bgs', v, w_if[2]) + b_if[None, :, None])
    hcell = mlstm_chunkwise(q, k, v, gates[:, :H], gates[:, H:])
    hcell = jnp.swapaxes(hcell, 1, 2)
    hn = rmsnorm(hcell, g_mlnorm.reshape(H, Dh))
    y = (hn + ml_skip.reshape(H, Dh) * xch) * jax.nn.sigmoid(o_pre).reshape(B, S, H, Dh)
    return y.reshape(B, S, H * Dh)


def setup_inputs(seed: int = 0) -> dict:
    key = jax.random.key(seed)
    ks = jax.random.split(key, 40)
    L = DEPTH

    def nrm(k, shape, scale):
        return jax.random.normal(k, shape, jnp.float32) * scale

    def gain(k, shape):
        return 1.0 + 0.05 * jax.random.normal(k, shape, jnp.float32)

    in_cols = 3 * DA_WIDTH + 2 * ML_WIDTH + 2 * D_MODEL
    f_bias = jnp.linspace(3.0, 6.0, ML_HEADS, dtype=jnp.float32)
    b_if = jnp.concatenate([nrm(ks[30], (L, ML_HEADS), 0.1),
                            f_bias[None, :] + nrm(ks[31], (L, ML_HEADS), 0.1)], axis=-1)
    return {
        "x": nrm(ks[0], (BATCH, SEQ, D_MODEL), 1.0),
        "c": nrm(ks[1], (BATCH, D_MODEL), 1.0),
        "w_ada": nrm(ks[2], (L, D_MODEL, N_MOD * D_MODEL), D_MODEL ** -0.5),
        "b_ada": nrm(ks[3], (L, N_MOD * D_MODEL), 0.02),
        "g_ff1": gain(ks[4], (L, D_MODEL)),
        "w1_gate": nrm(ks[5], (L, D_MODEL, D_FF), D_MODEL ** -0.5),
        "w1_up": nrm(ks[6], (L, D_MODEL, D_FF), D_MODEL ** -0.5),
        "w1_down": nrm(ks[7], (L, D_FF, D_MODEL), D_FF ** -0.5),
        "g_mix": gain(ks[8], (L, D_MODEL)),
        "w_in": nrm(ks[9], (L, D_MODEL, in_cols), D_MODEL ** -0.5),
        "lambda_q1": nrm(ks[10], (L, DA_HEAD_DIM), 0.1),
        "lambda_k1": nrm(ks[11], (L, DA_HEAD_DIM), 0.1),
        "lambda_q2": nrm(ks[12], (L, DA_HEAD_DIM), 0.1),
        "lambda_k2": nrm(ks[13], (L, DA_HEAD_DIM), 0.1),
        "g_subln": gain(ks[14], (L, 2 * DA_HEAD_DIM)),
        "conv_w": nrm(ks[15], (L, CONV_WIDTH, ML_WIDTH), CONV_WIDTH ** -0.5),
        "conv_b": nrm(ks[16], (L, ML_WIDTH), 0.02),
        "w_mq": nrm(ks[17], (L, ML_HEADS, ML_HEAD_DIM, ML_HEAD_DIM), ML_HEAD_DIM ** -0.5),
        "w_mk": nrm(ks[18], (L, ML_HEADS, ML_HEAD_DIM, ML_HEAD_DIM), ML_HEAD_DIM ** -0.5),
        "w_mv": nrm(ks[19], (L, ML_HEADS, ML_HEAD_DIM, ML_HEAD_DIM), ML_HEAD_DIM ** -0.5),
        "w_if": nrm(ks[20], (L, 3, ML_HEADS, ML_HEAD_DIM, 2 * ML_HEADS), (3 * ML_WIDTH) ** -0.5),
        "b_if": b_if,
        "ml_skip": gain(ks[21], (L, ML_WIDTH)),
        "g_mlnorm": gain(ks[22], (L, ML_WIDTH)),
        "w_proj_a": nrm(ks[23], (L, DA_WIDTH, D_MODEL), DA_WIDTH ** -0.5),
        "w_proj_b": nrm(ks[24], (L, ML_WIDTH, D_MODEL), ML_WIDTH ** -0.5),
        "w_out": nrm(ks[25], (L, D_MODEL, D_MODEL), D_MODEL ** -0.5),
        "g_ff2": gain(ks[26], (L, D_MODEL)),
        "w2_gate": nrm(ks[27], (L, D_MODEL, D_FF), D_MODEL ** -0.5),
        "w2_up": nrm(ks[28], (L, D_MODEL, D_FF), D_MODEL ** -0.5),
        "w2_down": nrm(ks[29], (L, D_FF, D_MODEL), D_FF ** -0.5),
        "g_final": gain(ks[32], (D_MODEL,)),
    }


def reference(x, c, w_ada, b_ada, g_ff1, w1_gate, w1_up, w1_down, g_mix, w_in,
              lambda_q1, lambda_k1, lambda_q2, lambda_k2, g_subln, conv_w, conv_b,
              w_mq, w_mk, w_mv, w_if, b_if, ml_skip, g_mlnorm, w_proj_a, w_proj_b,
              w_out, g_ff2, w2_gate, w2_up, w2_down, g_final):
    B, S, D = x.shape
    split_idx = list(np.cumsum([DA_WIDTH, DA_WIDTH, DA_WIDTH, ML_WIDTH, ML_WIDTH, D_MODEL])[:])
    h = x
    for l in range(DEPTH):
        lambda_init = 0.8 - 0.6 * math.exp(-0.3 * l)
        mods = jnp.split(jax.nn.silu(c) @ w_ada[l] + b_ada[l], N_MOD, axis=-1)
        sh1, sc1, gt1, sh2, sc2, gt2, sh3, sc3, gt3 = mods

        u = modulate(rmsnorm(h, g_ff1[l]), sh1, sc1)
        h = h + FFN_RES * gt1[:, None, :] * swiglu(u, w1_gate[l], w1_up[l], w1_down[l])

        u = modulate(rmsnorm(h, g_mix[l]), sh2, sc2)
        proj = u @ w_in[l]
        qa, ka, va, xm, o_pre, ga, gb = jnp.split(proj, split_idx, axis=-1)

        lam = (jnp.exp(jnp.sum(lambda_q1[l].astype(jnp.float32) * lambda_k1[l].astype(jnp.float32)))
               - jnp.exp(jnp.sum(lambda_q2[l].astype(jnp.float32) * lambda_k2[l].astype(jnp.float32)))
               + lambda_init)
        ya = diff_attention(qa.reshape(B, S, DA_HEADS, 2, DA_HEAD_DIM),
                            ka.reshape(B, S, DA_HEADS, 2, DA_HEAD_DIM),
                            va.reshape(B, S, DA_HEADS, 2 * DA_HEAD_DIM),
                            lam, g_subln[l], lambda_init)
        yb = mlstm_branch(xm, o_pre, conv_w[l], conv_b[l], w_mq[l], w_mk[l], w_mv[l],
                          w_if[l], b_if[l], ml_skip[l], g_mlnorm[l])
        merged = jax.nn.sigmoid(ga) * (ya @ w_proj_a[l]) + jax.nn.sigmoid(gb) * (yb @ w_proj_b[l])
        h = h + gt2[:, None, :] * (merged @ w_out[l])

        u = modulate(rmsnorm(h, g_ff2[l]), sh3, sc3)
        h = h + FFN_RES * gt3[:, None, :] * swiglu(u, w2_gate[l], w2_up[l], w2_down[l])
    return rmsnorm(h, g_final)
```

```python
import math
from contextlib import ExitStack
import numpy as np
import concourse.bass as bass
import concourse.mybir as mybir
from concourse.bass_utils import run_bass_kernel_spmd

F32 = mybir.dt.float32
BF16 = mybir.dt.bfloat16
AF = mybir.ActivationFunctionType
ALU = mybir.AluOpType

EPS = 1e-6
NW = 6
NSLOT = 108
LAMBDA_INIT = 0.8 - 0.6 * math.exp(-0.3 * 0)


class Reg:
    __slots__ = ("name", "w", "r")

    def __init__(self, name):
        self.name = name
        self.w = None
        self.r = {}


class Eng:
    def __init__(self, nc, name, h, is_pe=False):
        self.h = h
        self.name = name
        self.sem = nc.alloc_semaphore("sem_" + name)
        self.cnt = 0
        self.waited = {}
        self.is_pe = is_pe


class Tracker:
    def __init__(self, nc, n_dma_sems=56):
        self.nc = nc
        self.PE = Eng(nc, "pe", nc.tensor, True)
        self.ACT = Eng(nc, "act", nc.scalar)
        self.DVE = Eng(nc, "dve", nc.vector)
        self.POOL = Eng(nc, "pool", nc.gpsimd)
        self.SP = Eng(nc, "sp", nc.sync)
        self.engs = [self.PE, self.ACT, self.DVE, self.POOL, self.SP]
        self.dsems = [nc.alloc_semaphore(f"dsem{i}") for i in range(n_dma_sems)]
        self.dvals = [0] * n_dma_sems
        self.drr = 0
        self.n_inst = 0

    def _wait(self, eng, dep):
        sem, v = dep
        if sem is eng.sem and eng.is_pe:
            return
        key = sem.num
        if eng.waited.get(key, 0) >= v:
            return
        eng.h.wait_ge(sem, v)
        eng.waited[key] = v

    @staticmethod
    def _deps(R, W):
        deps = []
        for r in R:
            if r.w is not None:
                deps.append(r.w)
        for w in W:
            if w.w is not None:
                deps.append(w.w)
            deps.extend(w.r.values())
        return deps

    @staticmethod
    def _commit(d, R, W):
        for w in W:
            w.w = d
            w.r = {}
        for r in R:
            if r in W:
                continue
            k = d[0].num
            if k not in r.r or r.r[k][1] < d[1]:
                r.r[k] = d

    def op(self, eng, fn, R=(), W=()):
        for d in self._deps(R, W):
            self._wait(eng, d)
        inst = fn()
        inst.then_inc(eng.sem, 1)
        eng.cnt += 1
        self.n_inst += 1
        self._commit((eng.sem, eng.cnt), R, W)

    def dma(self, q, out, in_, R=(), W=(), **kw):
        for d in self._deps(R, W):
            self._wait(q, d)
        i = self.drr
        self.drr = (i + 1) % len(self.dsems)
        sem = self.dsems[i]
        prev = self.dvals[i]
        if prev > 0:
            self._wait(q, (sem, prev))
        q.h.dma_start(out=out, in_=in_, **kw).then_inc(sem, 16)
        self.dvals[i] = prev + 16
        self.n_inst += 1
        d = (sem, prev + 16)
        self._commit(d, R, W)
        return d

    def barrier(self):
        for e in self.engs:
            for x in self.engs:
                if x is not e and x.cnt > 0:
                    self._wait(e, (x.sem, x.cnt))
            for sem, v in zip(self.dsems, self.dvals):
                if v > 0:
                    self._wait(e, (sem, v))


def build(NT=8, dbg=()):
    nc = bass.Bass("TRN2", target_bir_lowering=False)
    _orig_sbuf_tensor = nc.sbuf_tensor
    _uid = [0]

    def _sbuf_tensor(name, shape, dtype):
        _uid[0] += 1
        return _orig_sbuf_tensor(f"ph_{name}_{_uid[0]}", shape, dtype)
    S = NT * 512
    T = Tracker(nc)
    PE, ACT, DVE, POOL, SP = T.PE, T.ACT, T.DVE, T.POOL, T.SP

    def din(name, shape, dtype=F32):
        return nc.dram_tensor(name, list(shape), dtype, kind="ExternalInput").ap()

    x_d = din("x", [S, 1024])
    cpm_d = din("cpm", [128, 8])
    wada_d = din("wada", [128, 8, 9216])
    bada_d = din("bada", [1, 9216])
    gpm_d = din("gpm", [128, 24])
    gfin_d = din("gfin", [1, 1024])
    lamv_d = din("lamv", [1, 256])
    gsub_d = din("gsub", [1, 128])
    convw_d = din("convw", [128, 32])
    cvec_d = din("cvec", [128, 24])
    wm_d = din("wm", [128, 3 * 4 * 2 * 256])
    wif_d = din("wif", [128, 3 * 4 * 2 * 8])
    bif_d = din("bif", [4, 2])
    cos_d = din("cos", [S, 32])
    sin_d = din("sin", [S, 32])
    cst_d = din("cst", [128, 128 + 64 + 512])
    ws_d = din("ws", [NSLOT, 128, 2048])
    out_d = nc.dram_tensor("out", [S, 1024], F32, kind="ExternalOutput").ap()
    kth_d = nc.dram_tensor("kth", [NT, 128, 8, 512], BF16, kind="Internal").ap()
    vth_d = nc.dram_tensor("vth", [NT, 128, 8, 4 * 129], BF16, kind="Internal").ap()
    dbg_out = {}

    def SB(name, shape, dtype):
        return nc.alloc_sbuf_tensor("sb_" + name, list(shape), dtype)

    h = SB("h", [128, 4, 1024], F32)
    hR = [[Reg(f"h{g}{f}") for f in range(2)] for g in range(4)]
    uT = SB("uT", [128, 8, 512], BF16)
    uTR = [Reg(f"uT{a}") for a in range(8)]
    wsl = [SB(f"wsl{i}", [128, 2048], BF16) for i in range(NW)]
    wslR = [Reg(f"wsl{i}") for i in range(NW)]
    gtb = [SB(f"gtb{i}", [128, 1024], F32) for i in range(3)]
    gtbR = [Reg(f"gtb{i}") for i in range(3)]
    gfinb = SB("gfinb", [128, 1024], F32)
    gsc = SB("gsc", [128, 3, 8], F32)
    shv = SB("shv", [128, 3, 8], F32)
    cR = Reg("consts")
    cst = SB("cst", [128, 704], F32)
    identF = cst[:, 0:128]
    maskT = cst[0:64, 128:192]
    identB_t = SB("identB", [128, 128], BF16)
    identB = identB_t[:]
    gsubb = SB("gsubb", [128, 128], F32)
    lamt = SB("lamt", [128, 8], F32)
    convw = SB("convw", [128, 8, 4], F32)
    cvec = SB("cvec", [128, 3, 8], F32)
    wm = SB("wm", [128, 3, 4, 2, 256], BF16)
    wif = SB("wif", [128, 3, 4, 2, 8], BF16)
    bif = SB("bif", [4, 4], F32)
    C32 = SB("C32", [128, 4, 2, 257], F32)
    Cb = SB("Cb", [128, 4, 2, 257], BF16)
    C32R = [[Reg(f"C32_{hd}{e}") for e in range(2)] for hd in range(4)]
    CbR = [[Reg(f"Cb_{hd}{e}") for e in range(2)] for hd in range(4)]
    hist = SB("hist", [128, 8, 3], F32)
    histR = [Reg(f"hist{a}") for a in range(8)]
    mcs = SB("mcs", [4, 16], F32)
    mcsR = Reg("mcs")
    cst_t = SB("cstile", [128, 4, 4, 32], F32)
    cstR = Reg("cstile")
    ss = SB("ss", [128, 8], F32)
    ssR = Reg("ss")
    junk = SB("junk", [128, 1024], BF16)
    junkR = Reg("junk")
    yaT = SB("yaT", [128, 8, 512], BF16)
    yaTR = [Reg(f"yaT{a}") for a in range(8)]
    ybT = SB("ybT", [128, 8, 512], BF16)
    ybTR = [Reg(f"ybT{a}") for a in range(8)]
    NTMP = 4
    tmpf = [SB(f"tmpf{i}", [128, 512], F32) for i in range(NTMP)]
    tmpfR = [Reg(f"tmpf{i}") for i in range(NTMP)]
    tmp_rr = [0]

    def tmp_get():
        i = tmp_rr[0]
        tmp_rr[0] = (i + 1) % NTMP
        return tmpf[i], tmpfR[i]

    banks = [nc.alloc_psum_tensor(f"ps{i}", [128, 512], F32) for i in range(8)]
    pR = [Reg(f"ps{i}") for i in range(8)]
    held = set()
    ps_rr = [0]

    def ps_get(hold=False):
        for _ in range(8):
            i = ps_rr[0]
            ps_rr[0] = (i + 1) % 8
            if i not in held:
                if hold:
                    held.add(i)
                return i
        raise RuntimeError("no free PSUM bank")

    def ps_rel(i):
        held.discard(i)

    def mm(out, lhsT, rhs, start, stop, R, W):
        T.op(PE, lambda: nc.tensor.matmul(out, lhsT=lhsT, rhs=rhs, start=start, stop=stop, skip_group_check=True), R, W)

    def tr(out, in_, ident, R, W):
        T.op(PE, lambda: nc.tensor.transpose(out=out, in_=in_, identity=ident), R, W)

    def act(out, in_, func, R, W, **kw):
        T.op(ACT, lambda: nc.scalar.activation(out=out, in_=in_, func=func, **kw), R, W)

    def tt(out, in0, in1, op, R, W):
        T.op(DVE, lambda: nc.vector.tensor_tensor(out=out, in0=in0, in1=in1, op=op), R, W)

    def ts(out, in0, s1, s2, op0, op1, R, W):
        if op1 is None:
            T.op(DVE, lambda: nc.vector.tensor_scalar(out=out, in0=in0, scalar1=s1, scalar2=None, op0=op0), R, W)
        else:
            T.op(DVE, lambda: nc.vector.tensor_scalar(out=out, in0=in0, scalar1=s1, scalar2=s2, op0=op0, op1=op1), R, W)

    def stt(out, in0, scalar, in1, op0, op1, R, W):
        T.op(DVE, lambda: nc.vector.scalar_tensor_tensor(out=out, in0=in0, scalar=scalar, in1=in1, op0=op0, op1=op1), R, W)

    def vcp(out, in_, R, W):
        T.op(DVE, lambda: nc.vector.tensor_copy(out=out, in_=in_), R, W)

    def acp(out, in_, R, W):
        T.op(ACT, lambda: nc.scalar.copy(out=out, in_=in_), R, W)

    cp_rr = [0]

    def cpy(out, in_, R, W):
        cp_rr[0] ^= 1
        (acp if cp_rr[0] else vcp)(out, in_, R, W)

    def recip(out, in_, R, W):
        T.op(DVE, lambda: nc.vector.reciprocal(out=out, in_=in_), R, W)

    def memset(ap, val, W):
        T.op(DVE, lambda: nc.vector.memset(ap, val), (), W)

    def dump(name, ap, shape, dtype, R):
        if name not in dbg:
            return
        d = nc.dram_tensor("dbg_" + name, list(shape), dtype, kind="ExternalOutput").ap()
        dbg_out[name] = d
        T.dma(SP, d, ap, R=R, W=[])

    wctr = [0, 0]

    def wnext():
        i = wctr[0] % NW
        T.dma(POOL, wsl[i][:], ws_d[wctr[1]], R=[], W=[wslR[i]])
        wctr[0] += 1
        wctr[1] = (wctr[1] + 1) % NSLOT
        return wsl[i], wslR[i]

    with ExitStack() as es:
        def PH(name, shape, dtype):
            return es.enter_context(_sbuf_tensor(name, list(shape), dtype))
        T.dma(SP, cst[:], cst_d[:, :], W=[cR])
        T.dma(SP, gfinb[:], gfin_d[0:1, :].to_broadcast([128, 1024]), W=[cR])
        T.dma(SP, gsubb[:], gsub_d[0:1, :].to_broadcast([128, 128]), W=[cR])
        T.dma(SP, convw[:].rearrange("p a j -> p (a j)"), convw_d[:, :], W=[cR])
        T.dma(SP, cvec[:].rearrange("p a j -> p (a j)"), cvec_d[:, :], W=[cR])
        T.dma(SP, bif[:, 0:2], bif_d[:, :], W=[cR])
        T.dma(POOL, wm[:].rearrange("p a b c d -> p a (b c d)"), wm_d[:, :].rearrange("p (a c) -> p a c", a=3), W=[cR])
        T.dma(POOL, wif[:].rearrange("p a b c d -> p (a b c d)"), wif_d[:, :], W=[cR])
        lamv = PH("lamv", [128, 256], F32)
        cpm = PH("cpm", [128, 8], F32)
        gpm = PH("gpm", [128, 3, 8], F32)
        crep = PH("crep", [128, 8, 128], F32)
        modsb = PH("modsb", [128, 9216], F32)
        modpm = PH("modpm", [128, 6, 8], F32)
        one11 = PH("one11", [1, 2], F32)
        wad = [PH(f"wad{i}", [128, 8, 256], F32) for i in range(2)]
        bad = [PH(f"bad{i}", [128, 256], F32) for i in range(2)]
        wadR = [Reg("wad0"), Reg("wad1")]
        pR0 = Reg("pro")
        T.dma(SP, lamv[:], lamv_d[0:1, :].to_broadcast([128, 256]), W=[pR0])
        T.dma(SP, cpm[:], cpm_d[:, :], W=[pR0])
        T.dma(SP, gpm[:].rearrange("p a j -> p (a j)"), gpm_d[:, :], W=[pR0])
        memset(one11[:], 1.0, [pR0])
        vcp(identB, identF, [cR], [cR])
        ts(gsubb[:], gsubb[:], 1.0 - LAMBDA_INIT, None, ALU.mult, None, [cR], [cR])
        ts(bif[:, 2:3], bif[:, 1:2], -1.0, None, ALU.mult, None, [cR], [cR])
        memset(C32[:].rearrange("p a b c -> p (a b c)"), 0.0, [cR])
        memset(Cb[:].rearrange("p a b c -> p (a b c)"), 0.0, [cR])
        memset(hist[:].rearrange("p a j -> p (a j)"), 0.0, [cR])
        memset(mcs[:], 0.0, [mcsR])
        tt(lamv[:, 0:64], lamv[:, 0:64], lamv[:, 64:128], ALU.mult, [pR0], [pR0])
        tt(lamv[:, 128:192], lamv[:, 128:192], lamv[:, 192:256], ALU.mult, [pR0], [pR0])
        T.op(DVE, lambda: nc.vector.tensor_reduce(out=lamt[:, 1:2], in_=lamv[:, 0:64], axis=mybir.AxisListType.X, op=ALU.add), [pR0], [cR])
        T.op(DVE, lambda: nc.vector.tensor_reduce(out=lamt[:, 2:3], in_=lamv[:, 128:192], axis=mybir.AxisListType.X, op=ALU.add), [pR0], [cR])
        act(lamt[:, 3:5], lamt[:, 1:3], AF.Exp, [cR], [cR])
        stt(lamt[:, 0:1], lamt[:, 4:5], -LAMBDA_INIT, lamt[:, 3:4], ALU.add, ALU.subtract, [cR], [cR])
        act(cpm[:], cpm[:], AF.Silu, [pR0], [pR0])
        for a in range(8):
            vcp(crep[:, a, :], cpm[:, a:a + 1].to_broadcast([128, 128]), [pR0], [pR0])
        for cb in range(36):
            i = cb % 2
            T.dma(SP, wad[i][:], wada_d[:, :, cb * 256:(cb + 1) * 256], W=[wadR[i]])
            T.dma(SP, bad[i][:], bada_d[0:1, cb * 256:(cb + 1) * 256].to_broadcast([128, 256]), W=[wadR[i]])
            b = ps_get()
            for a in range(8):
                mm(banks[b][:, 0:256], crep[:, a, :], wad[i][:, a, :], a == 0, a == 7, [pR0, wadR[i]], [pR[b]])
            tt(modsb[:, cb * 256:(cb + 1) * 256], banks[b][:, 0:256], bad[i][:], ALU.add, [pR[b], wadR[i]], [pR0])
        ts(gtb[0][:], modsb[:, 2048:3072], 0.5, None, ALU.mult, None, [pR0], [gtbR[0]])
        vcp(gtb[1][:], modsb[:, 5120:6144], [pR0], [gtbR[1]])
        ts(gtb[2][:], modsb[:, 8192:9216], 0.5, None, ALU.mult, None, [pR0], [gtbR[2]])
        b = ps_get()
        for vi, mi in enumerate((0, 1, 3, 4, 6, 7)):
            for a in range(8):
                col = 2 * (vi * 8 + a)
                mm(banks[b][:, col:col + 2], modsb[0:1, mi * 1024 + a * 128: mi * 1024 + (a + 1) * 128], one11[0:1, 0:2],
                   True, True, [pR0], [pR[b]])
        vcp(modpm[:].rearrange("p a j -> p (a j)"), banks[b][:, 0:96].rearrange("p (n two) -> p n two", two=2)[:, :, 0], [pR[b]], [pR0])
        for k in range(3):
            vcp(shv[:, k, :], modpm[:, 2 * k, :], [pR0], [cR])
            stt(gsc[:, k, :], modpm[:, 2 * k + 1, :], 1.0, gpm[:, k, :], ALU.add, ALU.mult, [pR0], [cR])
        T.barrier()

    sel = cst[0:4, 192:704].rearrange("p (h m) -> p h m", h=4)

    def norm_to_uT(k, xs, xsR):
        for g in range(4):
            act(junk[:], h[:, g, :], AF.Square, [hR[g][0], hR[g][1]], [junkR, ssR], accum_out=ss[:, g:g + 1])
        act(ss[:, 4:8], ss[:, 0:4], AF.Sqrt, [ssR], [ssR], scale=1.0 / 1024, bias=EPS)
        recip(ss[:, 0:4], ss[:, 4:8], [ssR], [ssR])
        for g in range(4):
            ts(xs[:, g, :], h[:, g, :], ss[:, g:g + 1], None, ALU.mult, None, [hR[g][0], hR[g][1], ssR], [xsR[g]])
        for a in range(8):
            b = ps_get()
            for g in range(4):
                tr(banks[b][:, g * 128:(g + 1) * 128], xs[:, g, a * 128:(a + 1) * 128], identF, [xsR[g], cR], [pR[b]])
            act(uT[:, a, :], banks[b][:], AF.Identity, [pR[b], cR], [uTR[a]], scale=gsc[:, k, a:a + 1], bias=shv[:, k, a:a + 1])

    def fm_chunks(src, srcR, npairs, evac):
        for i in range(npairs):
            wt, wr = wnext()
            wv = wt[:].rearrange("p (m a c) -> p m a c", m=2, a=8)
            bs = []
            for m in range(2):
                b = ps_get()
                for a in range(8):
                    mm(banks[b][:], wv[:, m, a, :], src[:, a, :], a == 0, a == 7, [wr, srcR[a]], [pR[b]])
                bs.append(b)
            evac(i, bs)

    def tm_proj(src, srcR, nk, ncb, evac):
        for cb in range(ncb):
            acc = [ps_get(hold=True) for _ in range(4)]
            for kg in range((nk + 3) // 4):
                wt, wr = wnext()
                wv = wt[:].rearrange("p (k c) -> p k c", k=4)
                for kk in range(4):
                    k = 4 * kg + kk
                    if k >= nk:
                        break
                    for g in range(4):
                        mm(banks[acc[g]][:], src[:, k, g * 128:(g + 1) * 128], wv[:, kk, :], k == 0, k == nk - 1,
                           [srcR[k], wr], [pR[acc[g]]])
            for g in range(4):
                evac(cb, g, acc[g])
                ps_rel(acc[g])

    def resid_evac(gi):
        def f(cb, g, b):
            tb, tbR = tmp_get()
            tt(tb[:], banks[b][:], gtb[gi][:, cb * 512:(cb + 1) * 512], ALU.mult, [pR[b], gtbR[gi]], [tbR])
            tt(h[:, g, cb * 512:(cb + 1) * 512], h[:, g, cb * 512:(cb + 1) * 512], tb[:], ALU.add, [tbR, hR[g][cb]], [hR[g][cb]])
        return f

    def ffn(k):
        with ExitStack() as es:
            xs = es.enter_context(_sbuf_tensor(f"xs", [128, 4, 1024], F32))
            gT = es.enter_context(_sbuf_tensor(f"gT", [128, 22, 512], BF16))
            xsR = [Reg(f"xs{g}") for g in range(4)]
            gTR = [Reg(f"gT{c}") for c in range(22)]
            norm_to_uT(k, xs, xsR)

            def ev(c, bs):
                sg, sgR = tmp_get()
                act(sg[:], banks[bs[0]][:], AF.Silu, [pR[bs[0]]], [sgR])
                tt(gT[:, c, :], sg[:], banks[bs[1]][:], ALU.mult, [sgR, pR[bs[1]]], [gTR[c]])
            fm_chunks(uT, uTR, 22, ev)
            tm_proj(gT, gTR, 22, 2, resid_evac(k))
            T.barrier()

    for t in range(NT):
        wctr[1] = 0
        r0 = t * 512
        T.dma(SP, h[:], x_d[r0:r0 + 512, :].rearrange("(g p) d -> p g d", p=128),
              W=[hR[g][f] for g in range(4) for f in range(2)])
        T.dma(SP, cst_t[:, 0, :, :], cos_d[r0:r0 + 512, :].rearrange("(g p) i -> p g i", p=128), W=[cstR])
        T.dma(SP, cst_t[:, 1, :, :], sin_d[r0:r0 + 512, :].rearrange("(g p) i -> p g i", p=128), W=[cstR])
        ts(cst_t[:, 2:4, :, :].rearrange("p a g i -> p (a g i)"), cst_t[:, 0:2, :, :].rearrange("p a g i -> p (a g i)"),
           0.125, None, ALU.mult, None, [cstR], [cstR])

        ffn(0)
        if t == 0:
            dump("h1", h[:], [128, 4, 1024], F32, [hR[g][f] for g in range(4) for f in range(2)])

        with ExitStack() as es:
            xs = es.enter_context(_sbuf_tensor("xs2", [128, 4, 1024], F32))
            xsR = [Reg(f"xs2{g}") for g in range(4)]
            norm_to_uT(1, xs, xsR)
            T.barrier()

        with ExitStack() as es:
            def PH(name, shape, dtype):
                return es.enter_context(_sbuf_tensor(name, list(shape), dtype))
            qr = PH("qr", [128, 4, 1024], BF16)
            kr = PH("kr", [128, 4, 1024], BF16)
            qrR = [Reg(f"qr{g}") for g in range(4)]
            krR = [Reg(f"kr{g}") for g in range(4)]
            QT = PH("QT", [128, 8, 512], BF16)
            QTR = [Reg(f"QT{i}") for i in range(8)]
            KTt = PH("KTt", [128, 8, 512], BF16)
            KTtR = Reg("KTt")
            Vt = PH("Vt", [128, 8, 4, 129], BF16)
            VtR = Reg("Vt")
            NKV = 2
            KTs = [PH(f"KTs{i}", [128, 512], BF16) for i in range(NKV)]
            Vs = [PH(f"Vs{i}", [128, 4, 129], BF16) for i in range(NKV)]
            kvR = [Reg(f"kv{i}") for i in range(NKV)]
            NPT = 4
            PT = [PH(f"PT{i}", [128, 512], BF16) for i in range(NPT)]
            PTR = [Reg(f"PT{i}") for i in range(NPT)]
            NPD = 3
            PD = [PH(f"PD{i}", [128, 512], BF16) for i in range(NPD)]
            PDR = [Reg(f"PD{i}") for i in range(NPD)]
            ya = PH("ya", [128, 4, 1024], BF16)
            yaR = [Reg(f"ya{g}") for g in range(4)]
            ra = PH("ra", [128, 8, 32], F32)
            rb = PH("rb", [128, 8, 32], F32)
            raR, rbR = Reg("ra"), Reg("rb")
            osm = PH("osm", [128, 16], F32)
            osmR = Reg("osm")
            o0 = PH("o0", [128, 128], F32)
            o0R = Reg("o0")
            memset(Vt[:, :, :, 128:129].rearrange("p a b c -> p (a b c)"), 1.0, [VtR])
            for i in range(NPD):
                memset(PD[i][:], 0.0, [PDR[i]])

            def qkv_evac(cb, g, b):
                if cb < 4:
                    dst, dstR = (qr, qrR) if cb < 2 else (kr, krR)
                    co, si = (2, 3) if cb < 2 else (0, 1)
                    c0 = (cb % 2) * 512
                    pv = banks[b][:].rearrange("p (n c i) -> p n c i", n=8, c=2)
                    dv = dst[:, g, c0:c0 + 512].rearrange("p (n c i) -> p n c i", n=8, c=2)
                    cosb = cst_t[:, co, g, :].unsqueeze(1).to_broadcast([128, 8, 32])
                    sinb = cst_t[:, si, g, :].unsqueeze(1).to_broadcast([128, 8, 32])
                    tt(ra[:], pv[:, :, 0, :], cosb, ALU.mult, [pR[b], cstR], [raR])
                    tt(rb[:], pv[:, :, 1, :], sinb, ALU.mult, [pR[b], cstR], [rbR])
                    tt(dv[:, :, 0, :], ra[:], rb[:], ALU.subtract, [raR, rbR], [dstR[g]])
                    tt(ra[:], pv[:, :, 0, :], sinb, ALU.mult, [pR[b], cstR], [raR])
                    tt(rb[:], pv[:, :, 1, :], cosb, ALU.mult, [pR[b], cstR], [rbR])
                    tt(dv[:, :, 1, :], ra[:], rb[:], ALU.add, [raR, rbR], [dstR[g]])
                else:
                    h0 = (cb - 4) * 4
                    cpy(Vt[:, h0:h0 + 4, g, 0:128], banks[b][:].rearrange("p (n d) -> p n d", n=4), [pR[b]], [VtR])
            tm_proj(uT, uTR, 8, 6, qkv_evac)

            for (src, srcR, dst, dR) in ((qr, qrR, QT, None), (kr, krR, KTt, KTtR)):
                for hh in range(8):
                    b = ps_get()
                    pb = banks[b][:].bitcast(BF16)
                    for g in range(4):
                        tr(pb[:, g * 128:(g + 1) * 128], src[:, g, hh * 128:(hh + 1) * 128], identB, [srcR[g], cR], [pR[b]])
                    cpy(dst[:, hh, :], pb[:, 0:512], [pR[b]], [QTR[hh] if dR is None else dR])
            kthR, vthR = Reg("kth"), Reg("vth")
            T.dma(SP, kth_d[t], KTt[:], R=[KTtR], W=[kthR])
            T.dma(SP, vth_d[t], Vt[:].rearrange("p a b c -> p a (b c)"), R=[VtR], W=[vthR])
            if t == 0:
                dump("QT", QT[:], [128, 8, 512], BF16, QTR)
                dump("KT", KTt[:], [128, 8, 512], BF16, [KTtR])
                dump("Vt", Vt[:], [128, 8, 4, 129], BF16, [VtR])

            kvi = 0
            pti = 0
            pdi = 0
            for hh in range(8):
                acc = [[ps_get(hold=True) for _ in range(2)] for _ in range(2)]
                accv = [[banks[acc[c][qp]][:, 0:258].rearrange("p (r c) -> p r c", r=2) for qp in range(2)] for c in range(2)]
                started = [[False, False], [False, False]]
                for j in range(t + 1):
                    s = kvi % NKV
                    kvi += 1
                    T.dma(SP, KTs[s][:], kth_d[j][:, hh, :], R=[kthR], W=[kvR[s]])
                    T.dma(SP, Vs[s][:].rearrange("p a c -> p (a c)"), vth_d[j][:, hh, :], R=[vthR], W=[kvR[s]])
                    for kb in range(4):
                        q0 = kb if j == t else 0
                        nq = (4 - q0) * 128
                        Bk = 4 * j + kb
                        for c in range(2):
                            b = ps_get()
                            mm(banks[b][:, 0:nq], KTs[s][c * 64:(c + 1) * 64, kb * 128:(kb + 1) * 128],
                               QT[c * 64:(c + 1) * 64, hh, q0 * 128:512], True, True, [kvR[s], QTR[hh]], [pR[b]])
                            if j == t:
                                p_t, p_r = PD[pdi % NPD], PDR[pdi % NPD]
                                pdi += 1
                                act(p_t[0:64, 0:nq], banks[b][0:64, 0:nq], AF.Exp, [pR[b]], [p_r])
                                if nq > 64:
                                    act(p_t[64:128, 64:nq], banks[b][64:128, 64:nq], AF.Exp, [pR[b]], [p_r])
                            else:
                                p_t, p_r = PT[pti % NPT], PTR[pti % NPT]
                                pti += 1
                                act(p_t[:, 0:nq], banks[b][:, 0:nq], AF.Exp, [pR[b]], [p_r])
                            for qb in range(q0, 4):
                                qp, r = qb // 2, qb % 2
                                st = not started[c][qp]
                                started[c][qp] = True
                                mm(accv[c][qp][:, r, :], p_t[:, (qb - q0) * 128:(qb - q0 + 1) * 128], Vs[s][:, kb, :],
                                   st, Bk == 4 * t + qb, [p_r, kvR[s]], [pR[acc[c][qp]]])
                for qb in range(4):
                    qp, r = qb // 2, qb % 2
                    a0, a1 = acc[0][qp], acc[1][qp]
                    recip(osm[:, 0:1], accv[0][qp][:, r, 128:129], [pR[a0]], [osmR])
                    recip(osm[:, 1:2], accv[1][qp][:, r, 128:129], [pR[a1]], [osmR])
                    tt(osm[:, 2:3], osm[:, 1:2], lamt[:, 0:1], ALU.mult, [osmR, cR], [osmR])
                    act(o0[:], accv[0][qp][:, r, 0:128], AF.Identity, [pR[a0], osmR], [o0R], scale=osm[:, 0:1])
                    stt(o0[:], accv[1][qp][:, r, 0:128], osm[:, 2:3], o0[:], ALU.mult, ALU.add, [pR[a1], osmR, o0R], [o0R])
                    act(junk[:, 0:128], o0[:], AF.Square, [o0R], [junkR, osmR], accum_out=osm[:, 3:4])
                    act(osm[:, 4:5], osm[:, 3:4], AF.Sqrt, [osmR], [osmR], scale=1.0 / 128, bias=EPS)
                    recip(osm[:, 5:6], osm[:, 4:5], [osmR], [osmR])
                    stt(ya[:, qb, hh * 128:(hh + 1) * 128], o0[:], osm[:, 5:6], gsubb[:], ALU.mult, ALU.mult, [o0R, osmR, cR], [yaR[qb]])
                for c in range(2):
                    for qp in range(2):
                        ps_rel(acc[c][qp])
            for a in range(8):
                b = ps_get()
                pb = banks[b][:].bitcast(BF16)
                for g in range(4):
                    tr(pb[:, g * 128:(g + 1) * 128], ya[:, g, a * 128:(a + 1) * 128], identB, [yaR[g], cR], [pR[b]])
                cpy(yaT[:, a, :], pb[:, 0:512], [pR[b]], [yaTR[a]])
            if t == 0:
                dump("yaT", yaT[:], [128, 8, 512], BF16, yaTR)
            T.barrier()

        with ExitStack() as es:
            def PH(name, shape, dtype):
                return es.enter_context(_sbuf_tensor(name, list(shape), dtype))
            xcb = PH("xcb", [128, 8, 512], BF16)
            xmb = PH("xmb", [128, 8, 512], BF16)
            xcbR = [Reg(f"xcb{a}") for a in range(8)]
            xmbR = [Reg(f"xmb{a}") for a in range(8)]
            sigo = PH("sigo", [128, 8, 512], BF16)
            sigoR = [Reg(f"sigo{a}") for a in range(8)]
            xm32 = [PH(f"xm32{i}", [128, 515], F32) for i in range(2)]
            xm32R = [Reg("xm320"), Reg("xm321")]
            tmpb = [PH(f"tmpb{i}", [128, 512], BF16) for i in range(3)]
            tmpbR = [Reg(f"tmpb{i}") for i in range(3)]
            rows = {n: PH("row_" + n, [4, 512], F32) for n in ("ig", "cs", "A", "Mx", "d", "eq", "ek", "wi", "dec", "emt", "one")}
            rowR = {n: Reg("row_" + n) for n in rows}
            ekT = PH("ekT", [64, 8, 4], F32)
            emT = PH("emT", [64, 8, 4], F32)
            ekTR, emTR = Reg("ekT"), Reg("emT")
            XB = {n: PH("XB_" + n, [128, 512], F32) for n in ("eq", "ek", "wi", "dec")}
            XBR = {n: Reg("XB_" + n) for n in XB}
            qtl = PH("qtl", [128, 2, 512], BF16)
            qht = PH("qht", [128, 2, 512], BF16)
            ktl = PH("ktl", [128, 2, 512], BF16)
            qtlR, qhtR, ktlR = Reg("qtl"), Reg("qht"), Reg("ktl")
            wk = PH("wk", [64, 8, 256], BF16)
            vaug = PH("vaug", [64, 8, 257], BF16)
            wkR = [Reg(f"wk{c}") for c in range(8)]
            vaR = [Reg(f"va{c}") for c in range(8)]
            scT = [PH(f"scT{i}", [64, 64], BF16) for i in range(2)]
            scTR = [Reg("scT0"), Reg("scT1")]
            hn = [PH(f"hn{i}", [64, 256], F32) for i in range(2)]
            hnR = [Reg("hn0"), Reg("hn1")]
            sm = [PH(f"sm{i}", [64, 8], F32) for i in range(2)]
            smR = [Reg("sm0"), Reg("sm1")]
            memset(vaug[:, :, 256:257].rearrange("p a c -> p (a c)"), 1.0, vaR)
            memset(rows["one"][:], 1.0, [rowR["one"]])

            def xm_evac(i, bs):
                for m in range(2):
                    a = 2 * i + m
                    b = bs[m]
                    xm, xmR = xm32[a % 2], xm32R[a % 2]
                    acp(xm[:, 3:515], banks[b][:], [pR[b]], [xmR])
                    vcp(xm[:, 0:3], hist[:, a, :], [histR[a]], [xmR])
                    tb, tbR = tmp_get()
                    ts(tb[:], xm[:, 0:512], convw[:, a, 0:1], cvec[:, 0, a:a + 1], ALU.mult, ALU.add, [xmR, cR], [tbR])
                    for j in range(1, 4):
                        stt(tb[:], xm[:, j:j + 512], convw[:, a, j:j + 1], tb[:], ALU.mult, ALU.add, [xmR, cR, tbR], [tbR])
                    vcp(hist[:, a, :], xm[:, 512:515], [xmR], [histR[a]])
                    act(xcb[:, a, :], tb[:], AF.Silu, [tbR], [xcbR[a]])
                    vcp(xmb[:, a, :], xm[:, 3:515], [xmR], [xmbR[a]])
            fm_chunks(uT, uTR, 4, xm_evac)

            def op_evac(i, bs):
                for m in range(2):
                    a = 2 * i + m
                    act(sigo[:, a, :], banks[bs[m]][:], AF.Sigmoid, [pR[bs[m]]], [sigoR[a]])
            fm_chunks(uT, uTR, 4, op_evac)

            def mproj(j, hd, ec, b):
                src, srcR = (xmb, xmbR) if j == 2 else (xcb, xcbR)
                for ap_ in range(2):
                    mm(banks[b][:], wm[:, j, hd, ap_, ec * 128:(ec + 1) * 128], src[:, 2 * hd + ap_, :], ap_ == 0, ap_ == 1,
                       [cR, srcR[2 * hd + ap_]], [pR[b]])
            bI = ps_get(hold=True)
            bF = ps_get(hold=True)
            n = 0
            for hd in range(4):
                for j in range(3):
                    for ec in range(2):
                        b = ps_get()
                        mproj(j, hd, ec, b)
                        tb_, tbR_ = tmpb[n % 3], tmpbR[n % 3]
                        cpy(tb_[:], banks[b][:], [pR[b]], [tbR_])
                        mm(banks[bI][0:4, :], wif[:, j, hd, ec, 0:4], tb_[:], n == 0, n == 23, [cR, tbR_], [pR[bI]])
                        mm(banks[bF][0:4, :], wif[:, j, hd, ec, 4:8], tb_[:], n == 0, n == 23, [cR, tbR_], [pR[bF]])
                        n += 1
            rw = rows
            ts(rw["ig"][:], banks[bI][0:4, :], bif[:, 0:1], None, ALU.add, None, [pR[bI], cR], [rowR["ig"]])
            act(rw["d"][:], banks[bF][0:4, :], AF.Exp, [pR[bF], cR], [rowR["d"]], scale=-1.0, bias=bif[:, 2:3])
            act(rw["d"][:], rw["d"][:], AF.Ln, [rowR["d"]], [rowR["d"]], bias=1.0)
            ps_rel(bI)
            ps_rel(bF)
            for ck in range(8):
                sl = slice(ck * 64, (ck + 1) * 64)
                T.op(DVE, lambda: nc.vector.tensor_tensor_scan(out=rw["cs"][:, sl], data0=rw["one"][:, sl], data1=rw["d"][:, sl],
                                                               initial=0.0, op0=ALU.mult, op1=ALU.add),
                     [rowR["one"], rowR["d"]], [rowR["cs"]])
            tt(rw["A"][:], rw["ig"][:], rw["cs"][:], ALU.add, [rowR["ig"], rowR["cs"]], [rowR["A"]])
            for ck in range(8):
                sl = slice(ck * 64, (ck + 1) * 64)
                T.op(DVE, lambda: nc.vector.tensor_tensor_scan(out=rw["Mx"][:, sl], data0=rw["one"][:, sl], data1=rw["A"][:, sl],
                                                               initial=mcs[:, ck:ck + 1], op0=ALU.mult, op1=ALU.max),
                     [rowR["one"], rowR["A"], mcsR], [rowR["Mx"]])
                tt(mcs[:, ck + 1:ck + 2], rw["Mx"][:, ck * 64 + 63:ck * 64 + 64], rw["cs"][:, ck * 64 + 63:ck * 64 + 64], ALU.subtract,
                   [rowR["Mx"], rowR["cs"]], [mcsR])
            v3 = lambda tl: tl[:].rearrange("p (c s) -> p c s", c=8)
            MxL = v3(rw["Mx"])[:, :, 63:64].to_broadcast([4, 8, 64])
            mcb = mcs[:, 0:8].unsqueeze(2).to_broadcast([4, 8, 64])
            tt(v3(rw["d"]), MxL, v3(rw["Mx"]), ALU.subtract, [rowR["Mx"]], [rowR["d"]])
            act(rw["eq"][:], rw["d"][:], AF.Exp, [rowR["d"]], [rowR["eq"]])
            tt(v3(rw["d"]), v3(rw["A"]), MxL, ALU.subtract, [rowR["Mx"], rowR["A"]], [rowR["d"]])
            act(rw["ek"][:], rw["d"][:], AF.Exp, [rowR["d"]], [rowR["ek"]])
            ts(rw["ek"][:], rw["ek"][:], 1.0 / 16, None, ALU.mult, None, [rowR["ek"]], [rowR["ek"]])
            tt(v3(rw["d"]), mcb, v3(rw["Mx"]), ALU.subtract, [rowR["Mx"], mcsR], [rowR["d"]])
            act(rw["wi"][:], rw["d"][:], AF.Exp, [rowR["d"]], [rowR["wi"]])
            tt(v3(rw["d"]), mcb, MxL, ALU.subtract, [rowR["Mx"], mcsR], [rowR["d"]])
            act(rw["dec"][:], rw["d"][:], AF.Exp, [rowR["d"]], [rowR["dec"]])
            tt(rw["d"][:], rw["cs"][:], rw["Mx"][:], ALU.subtract, [rowR["Mx"], rowR["cs"]], [rowR["d"]])
            act(rw["emt"][:], rw["d"][:], AF.Exp, [rowR["d"]], [rowR["emt"]])
            vcp(mcs[:, 0:1], mcs[:, 8:9], [mcsR], [mcsR])
            for (nm, dst, dR) in (("ek", ekT, ekTR), ("emt", emT, emTR)):
                b = ps_get()
                for ck in range(8):
                    tr(banks[b][0:64, ck * 4:(ck + 1) * 4], rw[nm][0:4, ck * 64:(ck + 1) * 64], identF[0:4, 0:4], [rowR[nm], cR], [pR[b]])
                vcp(dst[:].rearrange("p c h -> p (c h)"), banks[b][0:64, 0:32], [pR[b]], [dR])
            if t == 0:
                for nm in ("ig", "cs", "A", "Mx", "eq", "ek", "wi", "dec", "emt"):
                    dump("row_" + nm, rw[nm][:], [4, 512], F32, [rowR[nm]])
                dump("ekT", ekT[:], [64, 8, 4], F32, [ekTR])

            ci = 0
            for hd in range(4):
                for nm in ("eq", "ek", "wi", "dec"):
                    b = ps_get()
                    mm(banks[b][:], sel[:, hd, :], rw[nm][:], True, True, [cR, rowR[nm]], [pR[b]])
                    cpy(XB[nm][:], banks[b][:], [pR[b]], [XBR[nm]])
                for ec in range(2):
                    b = ps_get()
                    mproj(0, hd, ec, b)
                    tt(qtl[:, ec, :], banks[b][:], XB["eq"][:], ALU.mult, [pR[b], XBR["eq"]], [qtlR])
                    tt(qht[:, ec, :], banks[b][:], XB["wi"][:], ALU.mult, [pR[b], XBR["wi"]], [qhtR])
                    b = ps_get()
                    mproj(1, hd, ec, b)
                    tt(ktl[:, ec, :], banks[b][:], XB["ek"][:], ALU.mult, [pR[b], XBR["ek"]], [ktlR])
                for ck in range(0, 8, 2):
                    for (j, src, srcR) in ((1, xcb, xcbR), (2, xmb, xmbR)):
                        b = ps_get()
                        for r in range(2):
                            c_ = ck + r
                            for ap_ in range(2):
                                mm(banks[b][0:64, r * 256:(r + 1) * 256], src[:, 2 * hd + ap_, c_ * 64:(c_ + 1) * 64], wm[:, j, hd, ap_, :],
                                   (r == 0 and ap_ == 0), ap_ == 1, [cR, srcR[2 * hd + ap_]], [pR[b]])
                        for r in range(2):
                            c_ = ck + r
                            if j == 1:
                                ts(wk[:, c_, :], banks[b][0:64, r * 256:(r + 1) * 256], ekT[:, c_, hd:hd + 1], None, ALU.mult, None,
                                   [pR[b], ekTR], [wkR[c_]])
                            else:
                                acp(vaug[:, c_, 0:256], banks[b][0:64, r * 256:(r + 1) * 256], [pR[b]], [vaR[c_]])
                bT = [ps_get(hold=True) for _ in range(2)]
                for ck in range(8):
                    sl = slice(ck * 64, (ck + 1) * 64)
                    i2 = ci % 2
                    ci += 1
                    b = ps_get()
                    for ec in range(2):
                        mm(banks[b][0:64, 0:64], ktl[:, ec, sl], qtl[:, ec, sl], ec == 0, ec == 1, [ktlR, qtlR], [pR[b]])
                    tt(scT[i2][:], banks[b][0:64, 0:64], maskT, ALU.mult, [pR[b], cR], [scTR[i2]])
                    bn = ps_get()
                    mm(banks[bn][0:64, 0:257], scT[i2][:], vaug[:, ck, :], True, False, [scTR[i2], vaR[ck]], [pR[bn]])
                    for ec in range(2):
                        mm(banks[bn][0:64, 0:257], qht[:, ec, sl], Cb[:, hd, ec, :], False, ec == 1, [qhtR, CbR[hd][ec]], [pR[bn]])
                    for ec in range(2):
                        bc = ps_get()
                        mm(banks[bc][:, 0:257], wk[:, ck, ec * 128:(ec + 1) * 128], vaug[:, ck, :], True, True, [wkR[ck], vaR[ck]], [pR[bc]])
                        stt(C32[:, hd, ec, :], C32[:, hd, ec, :], XB["dec"][:, ck * 64:ck * 64 + 1], banks[bc][:, 0:257], ALU.mult, ALU.add,
                            [C32R[hd][ec], XBR["dec"], pR[bc]], [C32R[hd][ec]])
                        acp(Cb[:, hd, ec, :], C32[:, hd, ec, :], [C32R[hd][ec]], [CbR[hd][ec]])
                    s_, sR_ = sm[i2], smR[i2]
                    ts(s_[:, 0:1], banks[bn][0:64, 256:257], -1.0, emT[:, ck, hd:hd + 1], ALU.mult, ALU.max, [pR[bn], emTR], [sR_])
                    tt(s_[:, 1:2], s_[:, 0:1], banks[bn][0:64, 256:257], ALU.max, [sR_, pR[bn]], [sR_])
                    recip(s_[:, 2:3], s_[:, 1:2], [sR_], [sR_])
                    act(junk[0:64, 0:256], banks[bn][0:64, 0:256], AF.Square, [pR[bn], sR_], [junkR, sR_], scale=s_[:, 2:3], accum_out=s_[:, 3:4])
                    act(s_[:, 4:5], s_[:, 3:4], AF.Sqrt, [sR_], [sR_], scale=1.0 / 256, bias=EPS)
                    recip(s_[:, 5:6], s_[:, 4:5], [sR_], [sR_])
                    tt(s_[:, 6:7], s_[:, 5:6], s_[:, 2:3], ALU.mult, [sR_], [sR_])
                    act(hn[i2][:], banks[bn][0:64, 0:256], AF.Identity, [pR[bn], sR_], [hnR[i2]], scale=s_[:, 6:7])
                    for ec in range(2):
                        tr(banks[bT[ec]][:, sl], hn[i2][:, ec * 128:(ec + 1) * 128], identF[0:64, 0:64], [hnR[i2], cR], [pR[bT[ec]]])
                for ec in range(2):
                    a = 2 * hd + ec
                    tb, tbR = tmp_get()
                    ts(tb[:], xcb[:, a, :], cvec[:, 1, a:a + 1], None, ALU.mult, None, [xcbR[a], cR], [tbR])
                    stt(tb[:], banks[bT[ec]][:], cvec[:, 2, a:a + 1], tb[:], ALU.mult, ALU.add, [pR[bT[ec]], cR, tbR], [tbR])
                    tt(ybT[:, a, :], tb[:], sigo[:, a, :], ALU.mult, [tbR, sigoR[a]], [ybTR[a]])
                    ps_rel(bT[ec])
            if t == 0:
                dump("ybT", ybT[:], [128, 8, 512], BF16, ybTR)
            T.barrier()

        with ExitStack() as es:
            mT = es.enter_context(_sbuf_tensor("mT", [128, 8, 512], BF16))
            mTR = [Reg(f"mT{a}") for a in range(8)]
            for cc in range(8):
                wt, wr = wnext()
                wv = wt[:].rearrange("p (m a c) -> p m a c", m=2, a=8)
                sg = []
                for m in range(2):
                    b = ps_get()
                    for a in range(8):
                        mm(banks[b][:], wv[:, m, a, :], uT[:, a, :], a == 0, a == 7, [wr, uTR[a]], [pR[b]])
                    tb, tbR = tmp_get()
                    act(tb[:], banks[b][:], AF.Sigmoid, [pR[b]], [tbR])
                    sg.append((tb, tbR))
                wt, wr = wnext()
                wv = wt[:].rearrange("p (m a c) -> p m a c", m=2, a=8)
                for m, (src, srcR) in enumerate(((yaT, yaTR), (ybT, ybTR))):
                    b = ps_get()
                    for a in range(8):
                        mm(banks[b][:], wv[:, m, a, :], src[:, a, :], a == 0, a == 7, [wr, srcR[a]], [pR[b]])
                    tb, tbR = sg[m]
                    tt(tb[:], tb[:], banks[b][:], ALU.mult, [tbR, pR[b]], [tbR])
                tt(mT[:, cc, :], sg[0][0][:], sg[1][0][:], ALU.add, [sg[0][1], sg[1][1]], [mTR[cc]])
            tm_proj(mT, mTR, 8, 2, resid_evac(1))
            if t == 0:
                dump("h2", h[:], [128, 4, 1024], F32, [hR[g][f] for g in range(4) for f in range(2)])
            T.barrier()

        ffn(2)

        with ExitStack() as es:
            ot = es.enter_context(_sbuf_tensor("ot", [128, 4, 1024], F32))
            otR = [Reg(f"ot{g}") for g in range(4)]
            for g in range(4):
                act(junk[:], h[:, g, :], AF.Square, [hR[g][0], hR[g][1]], [junkR, ssR], accum_out=ss[:, g:g + 1])
            act(ss[:, 4:8], ss[:, 0:4], AF.Sqrt, [ssR], [ssR], scale=1.0 / 1024, bias=EPS)
            recip(ss[:, 0:4], ss[:, 4:8], [ssR], [ssR])
            for g in range(4):
                stt(ot[:, g, :], h[:, g, :], ss[:, g:g + 1], gfinb[:], ALU.mult, ALU.mult, [hR[g][0], hR[g][1], ssR, cR], [otR[g]])
            T.dma(SP, out_d[r0:r0 + 512, :].rearrange("(g p) d -> p g d", p=128), ot[:], R=otR, W=[])
            T.barrier()

    for sem, v in zip(T.dsems, T.dvals):
        if v > 0:
            T._wait(SP, (sem, v))
    return nc, dbg_out, T


def _fm2(Wa, Wb):
    def one(W):
        return W.reshape(8, 128, 128).transpose(1, 0, 2)
    return np.stack([one(Wa), one(Wb)], axis=1).reshape(128, 2048)


def _tm(W, k0, nk, c0):
    out = np.zeros((128, 4, 512), np.float32)
    for kk in range(min(4, nk - k0)):
        out[:, kk, :] = W[(k0 + kk) * 128:(k0 + kk + 1) * 128, c0:c0 + 512]
    return out.reshape(128, 2048)


def _ffn_slots(wg, wu, wd):
    sl = [_fm2(wg[:, c * 128:(c + 1) * 128], wu[:, c * 128:(c + 1) * 128]) for c in range(22)]
    for hf in range(2):
        for kg in range(6):
            sl.append(_tm(wd, 4 * kg, 22, hf * 512))
    return sl


def prep_shared(inp, S):
    f = lambda a: np.ascontiguousarray(np.asarray(a, dtype=np.float32))
    w_in = f(inp["w_in"][0])
    sl = _ffn_slots(f(inp["w1_gate"][0]), f(inp["w1_up"][0]), f(inp["w1_down"][0]))
    for cb in range(6):
        for ah in range(2):
            sl.append(_tm(w_in, 4 * ah, 8, cb * 512))
    for i in range(4):
        sl.append(_fm2(w_in[:, 3072 + 2 * i * 128:3072 + (2 * i + 1) * 128], w_in[:, 3072 + (2 * i + 1) * 128:3072 + (2 * i + 2) * 128]))
    for i in range(4):
        sl.append(_fm2(w_in[:, 4096 + 2 * i * 128:4096 + (2 * i + 1) * 128], w_in[:, 4096 + (2 * i + 1) * 128:4096 + (2 * i + 2) * 128]))
    wa, wb, wo = f(inp["w_proj_a"][0]), f(inp["w_proj_b"][0]), f(inp["w_out"][0])
    for cc in range(8):
        sl.append(_fm2(w_in[:, 5120 + cc * 128:5120 + (cc + 1) * 128], w_in[:, 6144 + cc * 128:6144 + (cc + 1) * 128]))
        sl.append(_fm2(wa[:, cc * 128:(cc + 1) * 128], wb[:, cc * 128:(cc + 1) * 128]))
    for hf in range(2):
        for ah in range(2):
            sl.append(_tm(wo, 4 * ah, 8, hf * 512))
    sl += _ffn_slots(f(inp["w2_gate"][0]), f(inp["w2_up"][0]), f(inp["w2_down"][0]))
    assert len(sl) == NSLOT
    ws = np.ascontiguousarray(np.stack(sl, axis=0))

    pm = lambda v: np.ascontiguousarray(f(v).reshape(8, 128).T)
    sh = {}
    sh["ws"] = ws
    sh["wada"] = np.ascontiguousarray(f(inp["w_ada"][0]).reshape(8, 128, 9216).transpose(1, 0, 2))
    sh["bada"] = f(inp["b_ada"][0]).reshape(1, 9216)
    sh["gpm"] = np.ascontiguousarray(np.stack([pm(inp["g_ff1"][0]), pm(inp["g_mix"][0]), pm(inp["g_ff2"][0])], axis=1).reshape(128, 24))
    sh["gfin"] = f(inp["g_final"]).reshape(1, 1024)
    sh["lamv"] = np.concatenate([f(inp[k][0]) for k in ("lambda_q1", "lambda_k1", "lambda_q2", "lambda_k2")]).reshape(1, 256)
    sh["gsub"] = (f(inp["g_subln"][0])).reshape(1, 128)
    cw = f(inp["conv_w"][0])
    sh["convw"] = np.ascontiguousarray(cw.reshape(4, 8, 128).transpose(2, 1, 0).reshape(128, 32))
    sh["cvec"] = np.ascontiguousarray(np.stack([pm(inp["conv_b"][0]), pm(inp["ml_skip"][0]), pm(inp["g_mlnorm"][0])], axis=1).reshape(128, 24))
    wmq = np.stack([f(inp["w_mq"][0]), f(inp["w_mk"][0]), f(inp["w_mv"][0])], axis=0)
    sh["wm"] = np.ascontiguousarray(wmq.reshape(3, 4, 2, 128, 256).transpose(3, 0, 1, 2, 4).reshape(128, 3 * 4 * 2 * 256))
    wif = f(inp["w_if"][0])
    sh["wif"] = np.ascontiguousarray(wif.reshape(3, 4, 2, 128, 8).transpose(3, 0, 1, 2, 4).reshape(128, 3 * 4 * 2 * 8))
    bif = f(inp["b_if"][0])
    sh["bif"] = np.ascontiguousarray(np.stack([bif[0:4], bif[4:8]], axis=1))
    inv = (np.float32(10000.0) ** (-(np.arange(0, 64, 2, dtype=np.float32)) / np.float32(64))).astype(np.float32)
    ang = (np.arange(S, dtype=np.float32)[:, None] * inv[None, :]).astype(np.float32)
    sh["cos"] = np.cos(ang).astype(np.float32)
    sh["sin"] = np.sin(ang).astype(np.float32)
    cst = np.zeros((128, 704), np.float32)
    cst[:, 0:128] = np.eye(128, dtype=np.float32)
    cst[0:64, 128:192] = np.triu(np.ones((64, 64), np.float32))
    for hd in range(4):
        cst[hd, 192 + hd * 128:192 + (hd + 1) * 128] = 1.0
    sh["cst"] = cst
    return sh


def prep_core(inp, b, S):
    f = lambda a: np.ascontiguousarray(np.asarray(a, dtype=np.float32))
    return {"x": f(inp["x"][b, :S]), "cpm": np.ascontiguousarray(f(inp["c"][b]).reshape(8, 128).T)}


_CACHE = {}


def kernel(**inputs):
    NT = 8
    S = NT * 512
    if "nc" not in _CACHE:
        _CACHE["nc"] = build(NT)[0]
    nc = _CACHE["nc"]
    sh = prep_shared(inputs, S)
    in_maps = []
    for b in range(8):
        m = dict(sh)
        m.update(prep_core(inputs, b, S))
        in_maps.append(m)
    res = run_bass_kernel_spmd(nc, in_maps, core_ids=list(range(8)))
    out = np.stack([np.asarray(res.results[b]["out"], dtype=np.float32) for b in range(8)], axis=0)
    return out
```

```python
import math
from contextlib import ExitStack
import numpy as np
import concourse.bass as bass
import concourse.mybir as mybir
from concourse.bass_utils import run_bass_kernel_spmd

F32 = mybir.dt.float32
BF16 = mybir.dt.bfloat16
AF = mybir.ActivationFunctionType
ALU = mybir.AluOpType

EPS = 1e-6
NW = 6
NSLOT = 108
LAMBDA_INIT = 0.8 - 0.6 * math.exp(-0.3 * 0)


class Reg:
    __slots__ = ("name", "w", "r")

    def __init__(self, name):
        self.name = name
        self.w = None
        self.r = {}


class Eng:
    def __init__(self, nc, name, h, is_pe=False):
        self.h = h
        self.name = name
        self.sem = nc.alloc_semaphore("sem_" + name)
        self.cnt = 0
        self.waited = {}
        self.is_pe = is_pe


class Tracker:
    def __init__(self, nc, n_dma_sems=56):
        self.nc = nc
        self.PE = Eng(nc, "pe", nc.tensor, True)
        self.ACT = Eng(nc, "act", nc.scalar)
        self.DVE = Eng(nc, "dve", nc.vector)
        self.POOL = Eng(nc, "pool", nc.gpsimd)
        self.SP = Eng(nc, "sp", nc.sync)
        self.engs = [self.PE, self.ACT, self.DVE, self.POOL, self.SP]
        self.dsems = [nc.alloc_semaphore(f"dsem{i}") for i in range(n_dma_sems)]
        self.dvals = [0] * n_dma_sems
        n_sw = 16
        self.pools = {"pool": list(range(0, n_sw)), "sp": list(range(n_sw, n_dma_sems))}
        self.drr = {"pool": 0, "sp": 0}
        self.n_inst = 0

    def _wait(self, eng, dep):
        sem, v = dep
        if sem is eng.sem and eng.is_pe:
            return
        key = sem.num
        if eng.waited.get(key, 0) >= v:
            return
        eng.h.wait_ge(sem, v)
        eng.waited[key] = v

    @staticmethod
    def _deps(R, W):
        deps = []
        for r in R:
            if r.w is not None:
                deps.append(r.w)
        for w in W:
            if w.w is not None:
                deps.append(w.w)
            deps.extend(w.r.values())
        return deps

    @staticmethod
    def _commit(d, R, W):
        for w in W:
            w.w = d
            w.r = {}
        for r in R:
            if r in W:
                continue
            k = d[0].num
            if k not in r.r or r.r[k][1] < d[1]:
                r.r[k] = d

    def op(self, eng, fn, R=(), W=()):
        for d in self._deps(R, W):
            self._wait(eng, d)
        inst = fn()
        inst.then_inc(eng.sem, 1)
        eng.cnt += 1
        self.n_inst += 1
        self._commit((eng.sem, eng.cnt), R, W)

    def dma(self, q, out, in_, R=(), W=(), **kw):
        for d in self._deps(R, W):
            self._wait(q, d)
        pl = self.pools[q.name]
        i = pl[self.drr[q.name]]
        self.drr[q.name] = (self.drr[q.name] + 1) % len(pl)
        sem = self.dsems[i]
        prev = self.dvals[i]
        if prev > 0:
            self._wait(q, (sem, prev))
        q.h.dma_start(out=out, in_=in_, **kw).then_inc(sem, 16)
        self.dvals[i] = prev + 16
        self.n_inst += 1
        d = (sem, prev + 16)
        self._commit(d, R, W)
        return d

    def barrier(self):
        for e in self.engs:
            for x in self.engs:
                if x is not e and x.cnt > 0:
                    self._wait(e, (x.sem, x.cnt))
            for sem, v in zip(self.dsems, self.dvals):
                if v > 0:
                    self._wait(e, (sem, v))


def build(NT=8, dbg=()):
    nc = bass.Bass("TRN2", target_bir_lowering=False)
    _orig_sbuf_tensor = nc.sbuf_tensor
    _uid = [0]

    def _sbuf_tensor(name, shape, dtype):
        _uid[0] += 1
        return _orig_sbuf_tensor(f"ph_{name}_{_uid[0]}", shape, dtype)
    S = NT * 512
    T = Tracker(nc)
    PE, ACT, DVE, POOL, SP = T.PE, T.ACT, T.DVE, T.POOL, T.SP

    def din(name, shape, dtype=F32):
        return nc.dram_tensor(name, list(shape), dtype, kind="ExternalInput").ap()

    x_d = din("x", [S, 1024])
    cpm_d = din("cpm", [128, 8])
    wada_d = din("wada", [36, 128, 2048])
    bada_d = din("bada", [1, 9216])
    gpm_d = din("gpm", [128, 24])
    gfin_d = din("gfin", [1, 1024])
    lamv_d = din("lamv", [1, 256])
    gsub_d = din("gsub", [1, 128])
    convw_d = din("convw", [128, 32])
    cvec_d = din("cvec", [128, 24])
    wm_d = din("wm", [128, 3 * 4 * 2 * 256])
    wif_d = din("wif", [128, 3 * 4 * 2 * 8])
    bif_d = din("bif", [4, 2])
    cos_d = din("cos", [S, 32])
    sin_d = din("sin", [S, 32])
    cst_d = din("cst", [128, 128 + 64 + 512])
    ws_d = din("ws", [NSLOT, 128, 2048])
    out_d = nc.dram_tensor("out", [S, 1024], F32, kind="ExternalOutput").ap()
    kth_d = nc.dram_tensor("kth", [NT, 128, 8, 512], BF16, kind="Internal").ap()
    vth_d = nc.dram_tensor("vth", [NT, 128, 8, 4 * 129], BF16, kind="Internal").ap()
    dbg_out = {}

    def SB(name, shape, dtype):
        return nc.alloc_sbuf_tensor("sb_" + name, list(shape), dtype)

    h = SB("h", [128, 4, 1024], F32)
    hR = [[Reg(f"h{g}{f}") for f in range(2)] for g in range(4)]
    uT = SB("uT", [128, 8, 512], BF16)
    uTR = [Reg(f"uT{a}") for a in range(8)]
    wsl = [SB(f"wsl{i}", [128, 2048], BF16) for i in range(NW)]
    wslR = [Reg(f"wsl{i}") for i in range(NW)]
    gtb = [SB(f"gtb{i}", [128, 1024], F32) for i in range(3)]
    gtbR = [Reg(f"gtb{i}") for i in range(3)]
    gfinb = SB("gfinb", [128, 1024], F32)
    gsc = SB("gsc", [128, 3, 8], F32)
    shv = SB("shv", [128, 3, 8], F32)
    cR = Reg("consts")
    cst = SB("cst", [128, 704], F32)
    identF = cst[:, 0:128]
    maskT = cst[0:64, 128:192]
    identB_t = SB("identB", [128, 128], BF16)
    identB = identB_t[:]
    gsubb = SB("gsubb", [128, 128], F32)
    lamt = SB("lamt", [128, 8], F32)
    convw = SB("convw", [128, 8, 4], F32)
    cvec = SB("cvec", [128, 3, 8], F32)
    wm = SB("wm", [128, 3, 4, 2, 256], BF16)
    wif = SB("wif", [128, 3, 4, 2, 8], BF16)
    bif = SB("bif", [4, 4], F32)
    C32 = SB("C32", [128, 4, 2, 257], F32)
    Cb = SB("Cb", [128, 4, 2, 257], BF16)
    C32R = [[Reg(f"C32_{hd}{e}") for e in range(2)] for hd in range(4)]
    CbR = [[Reg(f"Cb_{hd}{e}") for e in range(2)] for hd in range(4)]
    hist = SB("hist", [128, 8, 3], F32)
    histR = [Reg(f"hist{a}") for a in range(8)]
    mcs = SB("mcs", [4, 16], F32)
    mcsR = Reg("mcs")
    cst_t = SB("cstile", [128, 4, 4, 32], F32)
    cstR = Reg("cstile")
    ss = SB("ss", [128, 8], F32)
    ssR = Reg("ss")
    junk = SB("junk", [128, 1024], BF16)
    junkR = Reg("junk")
    yaT = SB("yaT", [128, 8, 512], BF16)
    yaTR = [Reg(f"yaT{a}") for a in range(8)]
    ybT = SB("ybT", [128, 8, 512], BF16)
    ybTR = [Reg(f"ybT{a}") for a in range(8)]
    NTMP = 4
    tmpf = [SB(f"tmpf{i}", [128, 512], F32) for i in range(NTMP)]
    tmpfR = [Reg(f"tmpf{i}") for i in range(NTMP)]
    tmp_rr = [0]

    def tmp_get():
        i = tmp_rr[0]
        tmp_rr[0] = (i + 1) % NTMP
        return tmpf[i], tmpfR[i]

    banks = [nc.alloc_psum_tensor(f"ps{i}", [128, 512], F32) for i in range(8)]
    pR = [Reg(f"ps{i}") for i in range(8)]
    held = set()
    ps_rr = [0]

    def ps_get(hold=False):
        for _ in range(8):
            i = ps_rr[0]
            ps_rr[0] = (i + 1) % 8
            if i not in held:
                if hold:
                    held.add(i)
                return i
        raise RuntimeError("no free PSUM bank")

    def ps_rel(i):
        held.discard(i)

    def mm(out, lhsT, rhs, start, stop, R, W):
        T.op(PE, lambda: nc.tensor.matmul(out, lhsT=lhsT, rhs=rhs, start=start, stop=stop, skip_group_check=True), R, W)

    def tr(out, in_, ident, R, W):
        T.op(PE, lambda: nc.tensor.transpose(out=out, in_=in_, identity=ident), R, W)

    def act(out, in_, func, R, W, **kw):
        T.op(ACT, lambda: nc.scalar.activation(out=out, in_=in_, func=func, **kw), R, W)

    def tt(out, in0, in1, op, R, W):
        T.op(DVE, lambda: nc.vector.tensor_tensor(out=out, in0=in0, in1=in1, op=op), R, W)

    def ts(out, in0, s1, s2, op0, op1, R, W):
        if op1 is None:
            T.op(DVE, lambda: nc.vector.tensor_scalar(out=out, in0=in0, scalar1=s1, scalar2=None, op0=op0), R, W)
        else:
            T.op(DVE, lambda: nc.vector.tensor_scalar(out=out, in0=in0, scalar1=s1, scalar2=s2, op0=op0, op1=op1), R, W)

    def stt(out, in0, scalar, in1, op0, op1, R, W):
        T.op(DVE, lambda: nc.vector.scalar_tensor_tensor(out=out, in0=in0, scalar=scalar, in1=in1, op0=op0, op1=op1), R, W)

    def vcp(out, in_, R, W):
        T.op(DVE, lambda: nc.vector.tensor_copy(out=out, in_=in_), R, W)

    def acp(out, in_, R, W):
        T.op(ACT, lambda: nc.scalar.copy(out=out, in_=in_), R, W)

    cp_rr = [0]

    def cpy(out, in_, R, W):
        cp_rr[0] ^= 1
        (acp if cp_rr[0] else vcp)(out, in_, R, W)

    def recip(out, in_, R, W):
        T.op(DVE, lambda: nc.vector.reciprocal(out=out, in_=in_), R, W)

    def memset(ap, val, W):
        T.op(DVE, lambda: nc.vector.memset(ap, val), (), W)

    def dump(name, ap, shape, dtype, R):
        if name not in dbg:
            return
        d = nc.dram_tensor("dbg_" + name, list(shape), dtype, kind="ExternalOutput").ap()
        dbg_out[name] = d
        T.dma(SP, d, ap, R=R, W=[])

    wctr = [0, 0]

    def wnext():
        i = wctr[0] % NW
        T.dma(POOL, wsl[i][:], ws_d[wctr[1]], R=[], W=[wslR[i]])
        wctr[0] += 1
        wctr[1] = (wctr[1] + 1) % NSLOT
        return wsl[i], wslR[i]

    with ExitStack() as es:
        es.enter_context(nc.named_scope("prologue"))

        def PH(name, shape, dtype):
            return es.enter_context(_sbuf_tensor(name, list(shape), dtype))
        T.dma(SP, cst[:], cst_d[:, :], W=[cR])
        T.dma(SP, gfinb[:], gfin_d[0:1, :].to_broadcast([128, 1024]), W=[cR])
        T.dma(SP, gsubb[:], gsub_d[0:1, :].to_broadcast([128, 128]), W=[cR])
        T.dma(SP, convw[:].rearrange("p a j -> p (a j)"), convw_d[:, :], W=[cR])
        T.dma(SP, cvec[:].rearrange("p a j -> p (a j)"), cvec_d[:, :], W=[cR])
        T.dma(SP, bif[:, 0:2], bif_d[:, :], W=[cR])
        T.dma(POOL, wm[:].rearrange("p a b c d -> p a (b c d)"), wm_d[:, :].rearrange("p (a c) -> p a c", a=3), W=[cR])
        T.dma(POOL, wif[:].rearrange("p a b c d -> p (a b c d)"), wif_d[:, :], W=[cR])
        lamv = PH("lamv", [128, 256], F32)
        cpm = PH("cpm", [128, 8], F32)
        gpm = PH("gpm", [128, 3, 8], F32)
        crep = PH("crep", [128, 8, 128], F32)
        modsb = PH("modsb", [128, 9216], F32)
        modpm = PH("modpm", [128, 6, 8], F32)
        one11 = PH("one11", [1, 2], F32)
        wad = [PH(f"wad{i}", [128, 8, 256], F32) for i in range(2)]
        bad = [PH(f"bad{i}", [128, 256], F32) for i in range(2)]
        wadR = [Reg("wad0"), Reg("wad1")]
        pR0 = Reg("pro")
        T.dma(SP, lamv[:], lamv_d[0:1, :].to_broadcast([128, 256]), W=[pR0])
        T.dma(SP, cpm[:], cpm_d[:, :], W=[pR0])
        T.dma(SP, gpm[:].rearrange("p a j -> p (a j)"), gpm_d[:, :], W=[pR0])
        memset(one11[:], 1.0, [pR0])
        vcp(identB, identF, [cR], [cR])
        ts(gsubb[:], gsubb[:], 1.0 - LAMBDA_INIT, None, ALU.mult, None, [cR], [cR])
        ts(bif[:, 2:3], bif[:, 1:2], -1.0, None, ALU.mult, None, [cR], [cR])
        memset(C32[:].rearrange("p a b c -> p (a b c)"), 0.0, [cR])
        memset(Cb[:].rearrange("p a b c -> p (a b c)"), 0.0, [cR])
        memset(hist[:].rearrange("p a j -> p (a j)"), 0.0, [cR])
        memset(mcs[:], 0.0, [mcsR])
        tt(lamv[:, 0:64], lamv[:, 0:64], lamv[:, 64:128], ALU.mult, [pR0], [pR0])
        tt(lamv[:, 128:192], lamv[:, 128:192], lamv[:, 192:256], ALU.mult, [pR0], [pR0])
        T.op(DVE, lambda: nc.vector.tensor_reduce(out=lamt[:, 1:2], in_=lamv[:, 0:64], axis=mybir.AxisListType.X, op=ALU.add), [pR0], [cR])
        T.op(DVE, lambda: nc.vector.tensor_reduce(out=lamt[:, 2:3], in_=lamv[:, 128:192], axis=mybir.AxisListType.X, op=ALU.add), [pR0], [cR])
        act(lamt[:, 3:5], lamt[:, 1:3], AF.Exp, [cR], [cR])
        stt(lamt[:, 0:1], lamt[:, 4:5], -LAMBDA_INIT, lamt[:, 3:4], ALU.add, ALU.subtract, [cR], [cR])
        act(cpm[:], cpm[:], AF.Silu, [pR0], [pR0])
        for a in range(8):
            vcp(crep[:, a, :], cpm[:, a:a + 1].to_broadcast([128, 128]), [pR0], [pR0])
        for cb in range(36):
            i = cb % 2
            T.dma(SP if cb % 2 == 0 else POOL, wad[i][:].rearrange("p a c -> p (a c)"), wada_d[cb], W=[wadR[i]])
            T.dma(SP, bad[i][:], bada_d[0:1, cb * 256:(cb + 1) * 256].to_broadcast([128, 256]), W=[wadR[i]])
            b = ps_get()
            for a in range(8):
                mm(banks[b][:, 0:256], crep[:, a, :], wad[i][:, a, :], a == 0, a == 7, [pR0, wadR[i]], [pR[b]])
            tt(modsb[:, cb * 256:(cb + 1) * 256], banks[b][:, 0:256], bad[i][:], ALU.add, [pR[b], wadR[i]], [pR0])
        ts(gtb[0][:], modsb[:, 2048:3072], 0.5, None, ALU.mult, None, [pR0], [gtbR[0]])
        vcp(gtb[1][:], modsb[:, 5120:6144], [pR0], [gtbR[1]])
        ts(gtb[2][:], modsb[:, 8192:9216], 0.5, None, ALU.mult, None, [pR0], [gtbR[2]])
        b = ps_get()
        for vi, mi in enumerate((0, 1, 3, 4, 6, 7)):
            for a in range(8):
                col = 2 * (vi * 8 + a)
                mm(banks[b][:, col:col + 2], modsb[0:1, mi * 1024 + a * 128: mi * 1024 + (a + 1) * 128], one11[0:1, 0:2],
                   True, True, [pR0], [pR[b]])
        vcp(modpm[:].rearrange("p a j -> p (a j)"), banks[b][:, 0:96].rearrange("p (n two) -> p n two", two=2)[:, :, 0], [pR[b]], [pR0])
        for k in range(3):
            vcp(shv[:, k, :], modpm[:, 2 * k, :], [pR0], [cR])
            stt(gsc[:, k, :], modpm[:, 2 * k + 1, :], 1.0, gpm[:, k, :], ALU.add, ALU.mult, [pR0], [cR])
        T.barrier()

    sel = cst[0:4, 192:704].rearrange("p (h m) -> p h m", h=4)

    def norm_to_uT(k, xs, xsR):
        for g in range(4):
            act(junk[:], h[:, g, :], AF.Square, [hR[g][0], hR[g][1]], [junkR, ssR], accum_out=ss[:, g:g + 1])
        act(ss[:, 4:8], ss[:, 0:4], AF.Sqrt, [ssR], [ssR], scale=1.0 / 1024, bias=EPS)
        recip(ss[:, 0:4], ss[:, 4:8], [ssR], [ssR])
        for g in range(4):
            ts(xs[:, g, :], h[:, g, :], ss[:, g:g + 1], None, ALU.mult, None, [hR[g][0], hR[g][1], ssR], [xsR[g]])
        for a in range(8):
            b = ps_get()
            for g in range(4):
                tr(banks[b][:, g * 128:(g + 1) * 128], xs[:, g, a * 128:(a + 1) * 128], identF, [xsR[g], cR], [pR[b]])
            act(uT[:, a, :], banks[b][:], AF.Identity, [pR[b], cR], [uTR[a]], scale=gsc[:, k, a:a + 1], bias=shv[:, k, a:a + 1])

    def fm_chunks(src, srcR, npairs, evac):
        for i in range(npairs):
            wt, wr = wnext()
            wv = wt[:].rearrange("p (m a c) -> p m a c", m=2, a=8)
            bs = []
            for m in range(2):
                b = ps_get()
                for a in range(8):
                    mm(banks[b][:], wv[:, m, a, :], src[:, a, :], a == 0, a == 7, [wr, srcR[a]], [pR[b]])
                bs.append(b)
            evac(i, bs)

    def tm_proj(src, srcR, nk, ncb, evac):
        for cb in range(ncb):
            acc = [ps_get(hold=True) for _ in range(4)]
            for kg in range((nk + 3) // 4):
                wt, wr = wnext()
                wv = wt[:].rearrange("p (k c) -> p k c", k=4)
                for kk in range(4):
                    k = 4 * kg + kk
                    if k >= nk:
                        break
                    for g in range(4):
                        mm(banks[acc[g]][:], src[:, k, g * 128:(g + 1) * 128], wv[:, kk, :], k == 0, k == nk - 1,
                           [srcR[k], wr], [pR[acc[g]]])
            for g in range(4):
                evac(cb, g, acc[g])
                ps_rel(acc[g])

    def resid_evac(gi):
        def f(cb, g, b):
            tb, tbR = tmp_get()
            tt(tb[:], banks[b][:], gtb[gi][:, cb * 512:(cb + 1) * 512], ALU.mult, [pR[b], gtbR[gi]], [tbR])
            tt(h[:, g, cb * 512:(cb + 1) * 512], h[:, g, cb * 512:(cb + 1) * 512], tb[:], ALU.add, [tbR, hR[g][cb]], [hR[g][cb]])
        return f

    def ffn(k):
        with ExitStack() as es:
            es.enter_context(nc.named_scope(f"ffn{k}"))
            xs = es.enter_context(_sbuf_tensor(f"xs", [128, 4, 1024], F32))
            gT = es.enter_context(_sbuf_tensor(f"gT", [128, 22, 512], BF16))
            xsR = [Reg(f"xs{g}") for g in range(4)]
            gTR = [Reg(f"gT{c}") for c in range(22)]
            norm_to_uT(k, xs, xsR)

            def ev(c, bs):
                sg, sgR = tmp_get()
                act(sg[:], banks[bs[0]][:], AF.Silu, [pR[bs[0]]], [sgR])
                tt(gT[:, c, :], sg[:], banks[bs[1]][:], ALU.mult, [sgR, pR[bs[1]]], [gTR[c]])
            fm_chunks(uT, uTR, 22, ev)
            tm_proj(gT, gTR, 22, 2, resid_evac(k))
            T.barrier()

    for t in range(NT):
        wctr[1] = 0
        r0 = t * 512
        T.dma(SP, h[:], x_d[r0:r0 + 512, :].rearrange("(g p) d -> p g d", p=128),
              W=[hR[g][f] for g in range(4) for f in range(2)])
        T.dma(SP, cst_t[:, 0, :, :], cos_d[r0:r0 + 512, :].rearrange("(g p) i -> p g i", p=128), W=[cstR])
        T.dma(SP, cst_t[:, 1, :, :], sin_d[r0:r0 + 512, :].rearrange("(g p) i -> p g i", p=128), W=[cstR])
        ts(cst_t[:, 2:4, :, :].rearrange("p a g i -> p (a g i)"), cst_t[:, 0:2, :, :].rearrange("p a g i -> p (a g i)"),
           0.125, None, ALU.mult, None, [cstR], [cstR])

        ffn(0)
        if t == 0:
            dump("h1", h[:], [128, 4, 1024], F32, [hR[g][f] for g in range(4) for f in range(2)])

        with ExitStack() as es:
            es.enter_context(nc.named_scope("norm2"))
            xs = es.enter_context(_sbuf_tensor("xs2", [128, 4, 1024], F32))
            xsR = [Reg(f"xs2{g}") for g in range(4)]
            norm_to_uT(1, xs, xsR)
            T.barrier()

        with ExitStack() as es:
            es.enter_context(nc.named_scope("attn"))
            def PH(name, shape, dtype):
                return es.enter_context(_sbuf_tensor(name, list(shape), dtype))
            qr = PH("qr", [128, 4, 1024], BF16)
            kr = PH("kr", [128, 4, 1024], BF16)
            qrR = [Reg(f"qr{g}") for g in range(4)]
            krR = [Reg(f"kr{g}") for g in range(4)]
            QT = PH("QT", [128, 8, 512], BF16)
            QTR = [Reg(f"QT{i}") for i in range(8)]
            KTt = PH("KTt", [128, 8, 512], BF16)
            KTtR = Reg("KTt")
            Vt = PH("Vt", [128, 8, 4, 129], BF16)
            VtR = Reg("Vt")
            NKV = 3
            KTs = [PH(f"KTs{i}", [128, 512], BF16) for i in range(NKV)]
            Vs = [PH(f"Vs{i}", [128, 4, 129], BF16) for i in range(NKV)]
            kvR = [Reg(f"kv{i}") for i in range(NKV)]
            NPT = 6
            PT = [PH(f"PT{i}", [128, 512], BF16) for i in range(NPT)]
            PTR = [Reg(f"PT{i}") for i in range(NPT)]
            NPD = 6
            PD = [PH(f"PD{i}", [128, 512], BF16) for i in range(NPD)]
            PDR = [Reg(f"PD{i}") for i in range(NPD)]
            ya = PH("ya", [128, 4, 1024], BF16)
            yaR = [Reg(f"ya{g}") for g in range(4)]
            ra = PH("ra", [128, 8, 32], F32)
            rb = PH("rb", [128, 8, 32], F32)
            raR, rbR = Reg("ra"), Reg("rb")
            oraw = [PH(f"oraw{i}", [128, 4, 2, 129], F32) for i in range(2)]
            orawR = [Reg("oraw0"), Reg("oraw1")]
            Oacc = [PH(f"Oacc{i}", [128, 4, 128], F32) for i in range(2)]
            OaccR = [Reg("Oacc0"), Reg("Oacc1")]
            rzs = [PH(f"rzs{i}", [128, 16], F32) for i in range(2)]
            rzsR = [Reg("rzs0"), Reg("rzs1")]
            memset(Vt[:, :, :, 128:129].rearrange("p a b c -> p (a b c)"), 1.0, [VtR])
            for i in range(NPD):
                memset(PD[i][:], 0.0, [PDR[i]])

            def qkv_evac(cb, g, b):
                if cb < 4:
                    dst, dstR = (qr, qrR) if cb < 2 else (kr, krR)
                    co, si = (2, 3) if cb < 2 else (0, 1)
                    c0 = (cb % 2) * 512
                    pv = banks[b][:].rearrange("p (n c i) -> p n c i", n=8, c=2)
                    dv = dst[:, g, c0:c0 + 512].rearrange("p (n c i) -> p n c i", n=8, c=2)
                    cosb = cst_t[:, co, g, :].unsqueeze(1).to_broadcast([128, 8, 32])
                    sinb = cst_t[:, si, g, :].unsqueeze(1).to_broadcast([128, 8, 32])
                    tt(ra[:], pv[:, :, 0, :], cosb, ALU.mult, [pR[b], cstR], [raR])
                    tt(rb[:], pv[:, :, 1, :], sinb, ALU.mult, [pR[b], cstR], [rbR])
                    tt(dv[:, :, 0, :], ra[:], rb[:], ALU.subtract, [raR, rbR], [dstR[g]])
                    tt(ra[:], pv[:, :, 0, :], sinb, ALU.mult, [pR[b], cstR], [raR])
                    tt(rb[:], pv[:, :, 1, :], cosb, ALU.mult, [pR[b], cstR], [rbR])
                    tt(dv[:, :, 1, :], ra[:], rb[:], ALU.add, [raR, rbR], [dstR[g]])
                else:
                    h0 = (cb - 4) * 4
                    cpy(Vt[:, h0:h0 + 4, g, 0:128], banks[b][:].rearrange("p (n d) -> p n d", n=4), [pR[b]], [VtR])
            tm_proj(uT, uTR, 8, 6, qkv_evac)

            for (src, srcR, dst, dR) in ((qr, qrR, QT, None), (kr, krR, KTt, KTtR)):
                for hh in range(8):
                    b = ps_get()
                    pb = banks[b][:].bitcast(BF16)
                    for g in range(4):
                        tr(pb[:, g * 128:(g + 1) * 128], src[:, g, hh * 128:(hh + 1) * 128], identB, [srcR[g], cR], [pR[b]])
                    cpy(dst[:, hh, :], pb[:, 0:512], [pR[b]], [QTR[hh] if dR is None else dR])
            kthR, vthR = Reg("kth"), Reg("vth")
            T.dma(SP, kth_d[t], KTt[:], R=[KTtR], W=[kthR])
            T.dma(SP, vth_d[t], Vt[:].rearrange("p a b c -> p a (b c)"), R=[VtR], W=[vthR])
            if t == 0:
                dump("QT", QT[:], [128, 8, 512], BF16, QTR)
                dump("KT", KTt[:], [128, 8, 512], BF16, [KTtR])
                dump("Vt", Vt[:], [128, 8, 4, 129], BF16, [VtR])

            kvi = 0
            pti = 0
            pdi = 0
            DEPTH = 2
            pending_epi = None
            for hh in range(8):
                acc = [[ps_get(hold=True) for _ in range(2)] for _ in range(2)]
                accv = [[banks[acc[c][qp]][:, 0:258].rearrange("p (r c) -> p r c", r=2) for qp in range(2)] for c in range(2)]
                started = [[False, False], [False, False]]
                blocks = [(j, kb) for j in range(t + 1) for kb in range(4)]
                kvslot = {}
                pend = []

                def stage1(blk):
                    nonlocal kvi, pti, pdi
                    j, kb = blk
                    if j not in kvslot:
                        s_ = kvi % NKV
                        kvi += 1
                        kvslot[j] = s_
                        T.dma(SP, KTs[s_][:], kth_d[j][:, hh, :], R=[kthR], W=[kvR[s_]])
                        T.dma(SP, Vs[s_][:].rearrange("p a c -> p (a c)"), vth_d[j][:, hh, :], R=[vthR], W=[kvR[s_]])
                    s_ = kvslot[j]
                    q0 = kb if j == t else 0
                    nq = (4 - q0) * 128
                    bs = []
                    for c in range(2):
                        b = ps_get()
                        mm(banks[b][:, 0:nq], KTs[s_][c * 64:(c + 1) * 64, kb * 128:(kb + 1) * 128],
                           QT[c * 64:(c + 1) * 64, hh, q0 * 128:512], True, True, [kvR[s_], QTR[hh]], [pR[b]])
                        bs.append(b)
                    pts = []
                    for c in range(2):
                        b = bs[c]
                        if j == t:
                            p_t, p_r = PD[pdi % NPD], PDR[pdi % NPD]
                            pdi += 1
                            act(p_t[0:64, 0:nq], banks[b][0:64, 0:nq], AF.Exp, [pR[b]], [p_r])
                            if nq > 64:
                                act(p_t[64:128, 64:nq], banks[b][64:128, 64:nq], AF.Exp, [pR[b]], [p_r])
                        else:
                            p_t, p_r = PT[pti % NPT], PTR[pti % NPT]
                            pti += 1
                            act(p_t[:, 0:nq], banks[b][:, 0:nq], AF.Exp, [pR[b]], [p_r])
                        pts.append((p_t, p_r))
                    return (pts, s_, q0)

                def stage2(blk, st1):
                    j, kb = blk
                    pts, s_, q0 = st1
                    Bk = 4 * j + kb
                    for c in range(2):
                        p_t, p_r = pts[c]
                        for qb in range(q0, 4):
                            qp, r = qb // 2, qb % 2
                            st = not started[c][qp]
                            started[c][qp] = True
                            mm(accv[c][qp][:, r, :], p_t[:, (qb - q0) * 128:(qb - q0 + 1) * 128], Vs[s_][:, kb, :],
                               st, Bk == 4 * t + qb, [p_r, kvR[s_]], [pR[acc[c][qp]]])

                for bi, blk in enumerate(blocks):
                    pend.append((blk, stage1(blk)))
                    if len(pend) > DEPTH:
                        stage2(*pend.pop(0))
                    if bi == 2 and pending_epi is not None:
                        pending_epi()
                        pending_epi = None
                while pend:
                    stage2(*pend.pop(0))
                if pending_epi is not None:
                    pending_epi()
                    pending_epi = None
                e_ = hh % 2
                orw, orwR, Oa, OaR, rz, rzR = oraw[e_], orawR[e_], Oacc[e_], OaccR[e_], rzs[e_], rzsR[e_]
                for c in range(2):
                    for qp in range(2):
                        vcp(orw[:, 2 * qp:2 * qp + 2, c, :], accv[c][qp], [pR[acc[c][qp]]], [orwR])
                        ps_rel(acc[c][qp])
                recip(rz[:, 0:8].rearrange("p (q c) -> p q c", c=2), orw[:, :, :, 128], [orwR], [rzR])
                rzv = rz[:, 0:8].rearrange("p (q c) -> p q c", c=2)
                ts(rzv[:, :, 1], rzv[:, :, 1], lamt[:, 0:1], None, ALU.mult, None, [rzR, cR], [rzR])
                tt(Oa[:], orw[:, :, 0, 0:128], rzv[:, :, 0:1].to_broadcast([128, 4, 128]), ALU.mult, [orwR, rzR], [OaR])
                tt(orw[:, :, 1, 0:128], orw[:, :, 1, 0:128], rzv[:, :, 1:2].to_broadcast([128, 4, 128]), ALU.mult, [orwR, rzR], [orwR])
                tt(Oa[:], Oa[:], orw[:, :, 1, 0:128], ALU.add, [orwR, OaR], [OaR])
                tt(orw[:, :, 0, 0:128], Oa[:], Oa[:], ALU.mult, [OaR], [orwR])
                T.op(DVE, lambda: nc.vector.tensor_reduce(out=rz[:, 8:12], in_=orw[:, :, 0, 0:128], axis=mybir.AxisListType.X, op=ALU.add),
                     [orwR], [rzR])

                def epi2(hh=hh, Oa=Oa, OaR=OaR, rz=rz, rzR=rzR):
                    act(rz[:, 12:16], rz[:, 8:12], AF.Ln, [rzR], [rzR], scale=1.0 / 128, bias=EPS)
                    act(rz[:, 12:16], rz[:, 12:16], AF.Exp, [rzR], [rzR], scale=-0.5)
                    tt(Oa[:], Oa[:], rz[:, 12:16].unsqueeze(2).to_broadcast([128, 4, 128]), ALU.mult, [OaR, rzR], [OaR])
                    tt(ya[:, :, hh * 128:(hh + 1) * 128], Oa[:], gsubb[:, :].unsqueeze(1).to_broadcast([128, 4, 128]), ALU.mult,
                       [OaR, cR], yaR)
                pending_epi = epi2
            if pending_epi is not None:
                pending_epi()
                pending_epi = None
            for a in range(8):
                b = ps_get()
                pb = banks[b][:].bitcast(BF16)
                for g in range(4):
                    tr(pb[:, g * 128:(g + 1) * 128], ya[:, g, a * 128:(a + 1) * 128], identB, [yaR[g], cR], [pR[b]])
                cpy(yaT[:, a, :], pb[:, 0:512], [pR[b]], [yaTR[a]])
            if t == 0:
                dump("yaT", yaT[:], [128, 8, 512], BF16, yaTR)
            T.barrier()

        with ExitStack() as es:
            es.enter_context(nc.named_scope("mlstm"))
            def PH(name, shape, dtype):
                return es.enter_context(_sbuf_tensor(name, list(shape), dtype))
            xcb = PH("xcb", [128, 8, 512], BF16)
            xmb = PH("xmb", [128, 8, 512], BF16)
            xcbR = [Reg(f"xcb{a}") for a in range(8)]
            xmbR = [Reg(f"xmb{a}") for a in range(8)]
            hraw = PH("hraw", [64, 8, 257], F32)
            hrawR = [Reg(f"hraw{c}") for c in range(8)]
            sm8 = PH("sm8", [64, 4, 8], F32)
            sm8R = Reg("sm8")
            xm32 = [PH(f"xm32{i}", [128, 515], F32) for i in range(4)]
            xm32R = [Reg(f"xm32{i}") for i in range(4)]
            tmpb = [PH(f"tmpb{i}", [128, 512], BF16) for i in range(3)]
            tmpbR = [Reg(f"tmpb{i}") for i in range(3)]
            rows = {n: PH("row_" + n, [4, 512], F32) for n in ("ig", "cs", "A", "Mx", "d", "eq", "ek", "wi", "dec", "emt", "one")}
            rowR = {n: Reg("row_" + n) for n in rows}
            ekT = PH("ekT", [64, 8, 4], F32)
            emT = PH("emT", [64, 8, 4], F32)
            ekTR, emTR = Reg("ekT"), Reg("emT")
            XB = {n: PH("XB_" + n, [128, 512], F32) for n in ("eq", "ek", "wi", "dec")}
            XBR = {n: Reg("XB_" + n) for n in XB}
            qtl = PH("qtl", [128, 2, 512], BF16)
            qht = PH("qht", [128, 2, 512], BF16)
            ktl = PH("ktl", [128, 2, 512], BF16)
            qtlR, qhtR, ktlR = Reg("qtl"), Reg("qht"), Reg("ktl")
            wk = PH("wk", [64, 8, 256], BF16)
            vaug = PH("vaug", [64, 8, 257], BF16)
            wkR = [Reg(f"wk{c}") for c in range(8)]
            vaR = [Reg(f"va{c}") for c in range(8)]
            scT = [PH(f"scT{i}", [64, 64], BF16) for i in range(2)]
            scTR = [Reg("scT0"), Reg("scT1")]
            hn = [PH(f"hn{i}", [64, 256], F32) for i in range(2)]
            hnR = [Reg("hn0"), Reg("hn1")]
            sm = [PH(f"sm{i}", [64, 8], F32) for i in range(2)]
            smR = [Reg("sm0"), Reg("sm1")]
            memset(vaug[:, :, 256:257].rearrange("p a c -> p (a c)"), 1.0, vaR)
            memset(rows["one"][:], 1.0, [rowR["one"]])

            for half in range(2):
                cbk = []
                for i2_ in range(2):
                    wt, wr = wnext()
                    wv = wt[:].rearrange("p (m a c) -> p m a c", m=2, a=8)
                    for m in range(2):
                        b = ps_get()
                        for a_ in range(8):
                            mm(banks[b][:], wv[:, m, a_, :], uT[:, a_, :], a_ == 0, a_ == 7, [wr, uTR[a_]], [pR[b]])
                        cbk.append((4 * half + 2 * i2_ + m, b))
                tbs = []
                for n_, (a, b) in enumerate(cbk):
                    acp(xm32[n_][:, 3:515], banks[b][:], [pR[b]], [xm32R[n_]])
                    vcp(xm32[n_][:, 0:3], hist[:, a, :], [histR[a]], [xm32R[n_]])
                    tbs.append(tmp_get())
                for n_, (a, b) in enumerate(cbk):
                    ts(tbs[n_][0][:], xm32[n_][:, 0:512], convw[:, a, 0:1], cvec[:, 0, a:a + 1], ALU.mult, ALU.add, [xm32R[n_], cR], [tbs[n_][1]])
                for j in range(1, 4):
                    for n_, (a, b) in enumerate(cbk):
                        stt(tbs[n_][0][:], xm32[n_][:, j:j + 512], convw[:, a, j:j + 1], tbs[n_][0][:], ALU.mult, ALU.add,
                            [xm32R[n_], cR, tbs[n_][1]], [tbs[n_][1]])
                for n_, (a, b) in enumerate(cbk):
                    vcp(hist[:, a, :], xm32[n_][:, 512:515], [xm32R[n_]], [histR[a]])
                    act(xcb[:, a, :], tbs[n_][0][:], AF.Silu, [tbs[n_][1]], [xcbR[a]])
                    vcp(xmb[:, a, :], xm32[n_][:, 3:515], [xm32R[n_]], [xmbR[a]])

            def mproj(j, hd, ec, b):
                src, srcR = (xmb, xmbR) if j == 2 else (xcb, xcbR)
                for ap_ in range(2):
                    mm(banks[b][:], wm[:, j, hd, ap_, ec * 128:(ec + 1) * 128], src[:, 2 * hd + ap_, :], ap_ == 0, ap_ == 1,
                       [cR, srcR[2 * hd + ap_]], [pR[b]])
            bI = ps_get(hold=True)
            bF = ps_get(hold=True)
            n = 0
            for hd in range(4):
                for j in range(3):
                    for ec in range(2):
                        b = ps_get()
                        mproj(j, hd, ec, b)
                        tb_, tbR_ = tmpb[n % 3], tmpbR[n % 3]
                        cpy(tb_[:], banks[b][:], [pR[b]], [tbR_])
                        mm(banks[bI][0:4, :], wif[:, j, hd, ec, 0:4], tb_[:], n == 0, n == 23, [cR, tbR_], [pR[bI]])
                        mm(banks[bF][0:4, :], wif[:, j, hd, ec, 4:8], tb_[:], n == 0, n == 23, [cR, tbR_], [pR[bF]])
                        n += 1
            rw = rows
            ts(rw["ig"][:], banks[bI][0:4, :], bif[:, 0:1], None, ALU.add, None, [pR[bI], cR], [rowR["ig"]])
            act(rw["d"][:], banks[bF][0:4, :], AF.Exp, [pR[bF], cR], [rowR["d"]], scale=-1.0, bias=bif[:, 2:3])
            act(rw["d"][:], rw["d"][:], AF.Ln, [rowR["d"]], [rowR["d"]], bias=1.0)
            ps_rel(bI)
            ps_rel(bF)
            for ck in range(8):
                sl = slice(ck * 64, (ck + 1) * 64)
                T.op(DVE, lambda: nc.vector.tensor_tensor_scan(out=rw["cs"][:, sl], data0=rw["one"][:, sl], data1=rw["d"][:, sl],
                                                               initial=0.0, op0=ALU.mult, op1=ALU.add),
                     [rowR["one"], rowR["d"]], [rowR["cs"]])
            tt(rw["A"][:], rw["ig"][:], rw["cs"][:], ALU.add, [rowR["ig"], rowR["cs"]], [rowR["A"]])
            for ck in range(8):
                sl = slice(ck * 64, (ck + 1) * 64)
                T.op(DVE, lambda: nc.vector.tensor_tensor_scan(out=rw["Mx"][:, sl], data0=rw["one"][:, sl], data1=rw["A"][:, sl],
                                                               initial=mcs[:, ck:ck + 1], op0=ALU.mult, op1=ALU.max),
                     [rowR["one"], rowR["A"], mcsR], [rowR["Mx"]])
                tt(mcs[:, ck + 1:ck + 2], rw["Mx"][:, ck * 64 + 63:ck * 64 + 64], rw["cs"][:, ck * 64 + 63:ck * 64 + 64], ALU.subtract,
                   [rowR["Mx"], rowR["cs"]], [mcsR])
            v3 = lambda tl: tl[:].rearrange("p (c s) -> p c s", c=8)
            MxL = v3(rw["Mx"])[:, :, 63:64].to_broadcast([4, 8, 64])
            mcb = mcs[:, 0:8].unsqueeze(2).to_broadcast([4, 8, 64])
            tt(v3(rw["d"]), MxL, v3(rw["Mx"]), ALU.subtract, [rowR["Mx"]], [rowR["d"]])
            act(rw["eq"][:], rw["d"][:], AF.Exp, [rowR["d"]], [rowR["eq"]])
            tt(v3(rw["d"]), v3(rw["A"]), MxL, ALU.subtract, [rowR["Mx"], rowR["A"]], [rowR["d"]])
            act(rw["ek"][:], rw["d"][:], AF.Exp, [rowR["d"]], [rowR["ek"]])
            ts(rw["ek"][:], rw["ek"][:], 1.0 / 16, None, ALU.mult, None, [rowR["ek"]], [rowR["ek"]])
            tt(v3(rw["d"]), mcb, v3(rw["Mx"]), ALU.subtract, [rowR["Mx"], mcsR], [rowR["d"]])
            act(rw["wi"][:], rw["d"][:], AF.Exp, [rowR["d"]], [rowR["wi"]])
            tt(v3(rw["d"]), mcb, MxL, ALU.subtract, [rowR["Mx"], mcsR], [rowR["d"]])
            act(rw["dec"][:], rw["d"][:], AF.Exp, [rowR["d"]], [rowR["dec"]])
            tt(rw["d"][:], rw["cs"][:], rw["Mx"][:], ALU.subtract, [rowR["Mx"], rowR["cs"]], [rowR["d"]])
            act(rw["emt"][:], rw["d"][:], AF.Exp, [rowR["d"]], [rowR["emt"]])
            vcp(mcs[:, 0:1], mcs[:, 8:9], [mcsR], [mcsR])
            for (nm, dst, dR) in (("ek", ekT, ekTR), ("emt", emT, emTR)):
                b = ps_get()
                for ck in range(8):
                    tr(banks[b][0:64, ck * 4:(ck + 1) * 4], rw[nm][0:4, ck * 64:(ck + 1) * 64], identF[0:4, 0:4], [rowR[nm], cR], [pR[b]])
                vcp(dst[:].rearrange("p c h -> p (c h)"), banks[b][0:64, 0:32], [pR[b]], [dR])
            if t == 0:
                for nm in ("ig", "cs", "A", "Mx", "eq", "ek", "wi", "dec", "emt"):
                    dump("row_" + nm, rw[nm][:], [4, 512], F32, [rowR[nm]])
                dump("ekT", ekT[:], [64, 8, 4], F32, [ekTR])

            ci = 0
            for hd in range(4):
                for nm in ("eq", "ek", "wi", "dec"):
                    b = ps_get()
                    mm(banks[b][:], sel[:, hd, :], rw[nm][:], True, True, [cR, rowR[nm]], [pR[b]])
                    cpy(XB[nm][:], banks[b][:], [pR[b]], [XBR[nm]])
                for ec in range(2):
                    b = ps_get()
                    mproj(0, hd, ec, b)
                    tt(qtl[:, ec, :], banks[b][:], XB["eq"][:], ALU.mult, [pR[b], XBR["eq"]], [qtlR])
                    tt(qht[:, ec, :], banks[b][:], XB["wi"][:], ALU.mult, [pR[b], XBR["wi"]], [qhtR])
                    b = ps_get()
                    mproj(1, hd, ec, b)
                    tt(ktl[:, ec, :], banks[b][:], XB["ek"][:], ALU.mult, [pR[b], XBR["ek"]], [ktlR])
                for ck in range(0, 8, 2):
                    for (j, src, srcR) in ((1, xcb, xcbR), (2, xmb, xmbR)):
                        b = ps_get()
                        for r in range(2):
                            c_ = ck + r
                            for ap_ in range(2):
                                mm(banks[b][0:64, r * 256:(r + 1) * 256], src[:, 2 * hd + ap_, c_ * 64:(c_ + 1) * 64], wm[:, j, hd, ap_, :],
                                   (r == 0 and ap_ == 0), ap_ == 1, [cR, srcR[2 * hd + ap_]], [pR[b]])
                        for r in range(2):
                            c_ = ck + r
                            if j == 1:
                                ts(wk[:, c_, :], banks[b][0:64, r * 256:(r + 1) * 256], ekT[:, c_, hd:hd + 1], None, ALU.mult, None,
                                   [pR[b], ekTR], [wkR[c_]])
                            else:
                                acp(vaug[:, c_, 0:256], banks[b][0:64, r * 256:(r + 1) * 256], [pR[b]], [vaR[c_]])
                bT = [ps_get(hold=True) for _ in range(2)]
                for ck in range(8):
                    sl = slice(ck * 64, (ck + 1) * 64)
                    i2 = ci % 2
                    ci += 1
                    b = ps_get()
                    for ec in range(2):
                        mm(banks[b][0:64, 0:64], ktl[:, ec, sl], qtl[:, ec, sl], ec == 0, ec == 1, [ktlR, qtlR], [pR[b]])
                    tt(scT[i2][:], banks[b][0:64, 0:64], maskT, ALU.mult, [pR[b], cR], [scTR[i2]])
                    bn = ps_get()
                    mm(banks[bn][0:64, 0:257], scT[i2][:], vaug[:, ck, :], True, False, [scTR[i2], vaR[ck]], [pR[bn]])
                    for ec in range(2):
                        mm(banks[bn][0:64, 0:257], qht[:, ec, sl], Cb[:, hd, ec, :], False, ec == 1, [qhtR, CbR[hd][ec]], [pR[bn]])
                    for ec in range(2):
                        bc = ps_get()
                        mm(banks[bc][:, 0:257], wk[:, ck, ec * 128:(ec + 1) * 128], vaug[:, ck, :], True, True, [wkR[ck], vaR[ck]], [pR[bc]])
                        stt(Cb[:, hd, ec, :], C32[:, hd, ec, :], XB["dec"][:, ck * 64:ck * 64 + 1], banks[bc][:, 0:257], ALU.mult, ALU.add,
                            [C32R[hd][ec], XBR["dec"], pR[bc]], [CbR[hd][ec]])
                        stt(C32[:, hd, ec, :], C32[:, hd, ec, :], XB["dec"][:, ck * 64:ck * 64 + 1], banks[bc][:, 0:257], ALU.mult, ALU.add,
                            [C32R[hd][ec], XBR["dec"], pR[bc]], [C32R[hd][ec]])
                    acp(hraw[:, ck, :], banks[bn][0:64, 0:257], [pR[bn]], [hrawR[ck]])
                den = hraw[:, :, 256]
                hv = hraw[:, :, 0:256]
                stt(sm8[:, 0, :], den, -1.0, emT[:, :, hd], ALU.mult, ALU.max, hrawR + [emTR], [sm8R])
                tt(sm8[:, 1, :], sm8[:, 0, :], den, ALU.max, hrawR + [sm8R], [sm8R])
                recip(sm8[:, 0, :], sm8[:, 1, :], [sm8R], [sm8R])
                tt(hv, hv, sm8[:, 0, :].unsqueeze(2).to_broadcast([64, 8, 256]), ALU.mult, hrawR + [sm8R], hrawR)
                for ck in range(8):
                    act(junk[0:64, 0:256], hraw[:, ck, 0:256], AF.Square, [hrawR[ck]], [junkR, sm8R], accum_out=sm8[:, 2, ck:ck + 1])
                act(sm8[:, 3, :], sm8[:, 2, :], AF.Sqrt, [sm8R], [sm8R], scale=1.0 / 256, bias=EPS)
                recip(sm8[:, 1, :], sm8[:, 3, :], [sm8R], [sm8R])
                tt(hv, hv, sm8[:, 1, :].unsqueeze(2).to_broadcast([64, 8, 256]), ALU.mult, hrawR + [sm8R], hrawR)
                for ck in range(8):
                    for ec in range(2):
                        tr(banks[bT[ec]][:, ck * 64:(ck + 1) * 64], hraw[:, ck, ec * 128:(ec + 1) * 128], identF[0:64, 0:64],
                           [hrawR[ck], cR], [pR[bT[ec]]])
                wt, wr = wnext()
                wv = wt[:].rearrange("p (m a c) -> p m a c", m=2, a=8)
                sgo = []
                for m_ in range(2):
                    b = ps_get()
                    for a_ in range(8):
                        mm(banks[b][:], wv[:, m_, a_, :], uT[:, a_, :], a_ == 0, a_ == 7, [wr, uTR[a_]], [pR[b]])
                    tb_, tbR_ = tmp_get()
                    act(tb_[:], banks[b][:], AF.Sigmoid, [pR[b]], [tbR_])
                    sgo.append((tb_, tbR_))
                for ec in range(2):
                    a = 2 * hd + ec
                    tb, tbR = tmp_get()
                    ts(tb[:], xcb[:, a, :], cvec[:, 1, a:a + 1], None, ALU.mult, None, [xcbR[a], cR], [tbR])
                    stt(tb[:], banks[bT[ec]][:], cvec[:, 2, a:a + 1], tb[:], ALU.mult, ALU.add, [pR[bT[ec]], cR, tbR], [tbR])
                    tt(ybT[:, a, :], tb[:], sgo[ec][0][:], ALU.mult, [tbR, sgo[ec][1]], [ybTR[a]])
                    ps_rel(bT[ec])
            if t == 0:
                dump("ybT", ybT[:], [128, 8, 512], BF16, ybTR)
            T.barrier()

        with ExitStack() as es:
            es.enter_context(nc.named_scope("merge"))
            mT = es.enter_context(_sbuf_tensor("mT", [128, 8, 512], BF16))
            mTR = [Reg(f"mT{a}") for a in range(8)]
            for cc in range(8):
                wt, wr = wnext()
                wv = wt[:].rearrange("p (m a c) -> p m a c", m=2, a=8)
                sg = []
                for m in range(2):
                    b = ps_get()
                    for a in range(8):
                        mm(banks[b][:], wv[:, m, a, :], uT[:, a, :], a == 0, a == 7, [wr, uTR[a]], [pR[b]])
                    tb, tbR = tmp_get()
                    act(tb[:], banks[b][:], AF.Sigmoid, [pR[b]], [tbR])
                    sg.append((tb, tbR))
                wt, wr = wnext()
                wv = wt[:].rearrange("p (m a c) -> p m a c", m=2, a=8)
                for m, (src, srcR) in enumerate(((yaT, yaTR), (ybT, ybTR))):
                    b = ps_get()
                    for a in range(8):
                        mm(banks[b][:], wv[:, m, a, :], src[:, a, :], a == 0, a == 7, [wr, srcR[a]], [pR[b]])
                    tb, tbR = sg[m]
                    tt(tb[:], tb[:], banks[b][:], ALU.mult, [tbR, pR[b]], [tbR])
                tt(mT[:, cc, :], sg[0][0][:], sg[1][0][:], ALU.add, [sg[0][1], sg[1][1]], [mTR[cc]])
            tm_proj(mT, mTR, 8, 2, resid_evac(1))
            if t == 0:
                dump("h2", h[:], [128, 4, 1024], F32, [hR[g][f] for g in range(4) for f in range(2)])
            T.barrier()

        ffn(2)

        with ExitStack() as es:
            es.enter_context(nc.named_scope("final"))
            ot = es.enter_context(_sbuf_tensor("ot", [128, 4, 1024], F32))
            otR = [Reg(f"ot{g}") for g in range(4)]
            for g in range(4):
                act(junk[:], h[:, g, :], AF.Square, [hR[g][0], hR[g][1]], [junkR, ssR], accum_out=ss[:, g:g + 1])
            act(ss[:, 4:8], ss[:, 0:4], AF.Sqrt, [ssR], [ssR], scale=1.0 / 1024, bias=EPS)
            recip(ss[:, 0:4], ss[:, 4:8], [ssR], [ssR])
            for g in range(4):
                stt(ot[:, g, :], h[:, g, :], ss[:, g:g + 1], gfinb[:], ALU.mult, ALU.mult, [hR[g][0], hR[g][1], ssR, cR], [otR[g]])
            T.dma(SP, out_d[r0:r0 + 512, :].rearrange("(g p) d -> p g d", p=128), ot[:], R=otR, W=[])
            T.barrier()

    for sem, v in zip(T.dsems, T.dvals):
        if v > 0:
            T._wait(SP, (sem, v))
    return nc, dbg_out, T


def _fm2(Wa, Wb):
    def one(W):
        return W.reshape(8, 128, 128).transpose(1, 0, 2)
    return np.stack([one(Wa), one(Wb)], axis=1).reshape(128, 2048)


def _tm(W, k0, nk, c0):
    out = np.zeros((128, 4, 512), np.float32)
    for kk in range(min(4, nk - k0)):
        out[:, kk, :] = W[(k0 + kk) * 128:(k0 + kk + 1) * 128, c0:c0 + 512]
    return out.reshape(128, 2048)


def _ffn_slots(wg, wu, wd):
    sl = [_fm2(wg[:, c * 128:(c + 1) * 128], wu[:, c * 128:(c + 1) * 128]) for c in range(22)]
    for hf in range(2):
        for kg in range(6):
            sl.append(_tm(wd, 4 * kg, 22, hf * 512))
    return sl


def prep_shared(inp, S):
    f = lambda a: np.ascontiguousarray(np.asarray(a, dtype=np.float32))
    w_in = f(inp["w_in"][0])
    sl = _ffn_slots(f(inp["w1_gate"][0]), f(inp["w1_up"][0]), f(inp["w1_down"][0]))
    for cb in range(6):
        for ah in range(2):
            sl.append(_tm(w_in, 4 * ah, 8, cb * 512))
    for i in range(4):
        sl.append(_fm2(w_in[:, 3072 + 2 * i * 128:3072 + (2 * i + 1) * 128], w_in[:, 3072 + (2 * i + 1) * 128:3072 + (2 * i + 2) * 128]))
    for i in range(4):
        sl.append(_fm2(w_in[:, 4096 + 2 * i * 128:4096 + (2 * i + 1) * 128], w_in[:, 4096 + (2 * i + 1) * 128:4096 + (2 * i + 2) * 128]))
    wa, wb, wo = f(inp["w_proj_a"][0]), f(inp["w_proj_b"][0]), f(inp["w_out"][0])
    for cc in range(8):
        sl.append(_fm2(w_in[:, 5120 + cc * 128:5120 + (cc + 1) * 128], w_in[:, 6144 + cc * 128:6144 + (cc + 1) * 128]))
        sl.append(_fm2(wa[:, cc * 128:(cc + 1) * 128], wb[:, cc * 128:(cc + 1) * 128]))
    for hf in range(2):
        for ah in range(2):
            sl.append(_tm(wo, 4 * ah, 8, hf * 512))
    sl += _ffn_slots(f(inp["w2_gate"][0]), f(inp["w2_up"][0]), f(inp["w2_down"][0]))
    assert len(sl) == NSLOT
    ws = np.ascontiguousarray(np.stack(sl, axis=0))

    pm = lambda v: np.ascontiguousarray(f(v).reshape(8, 128).T)
    sh = {}
    sh["ws"] = ws
    sh["wada"] = np.ascontiguousarray(f(inp["w_ada"][0]).reshape(8, 128, 36, 256).transpose(2, 1, 0, 3).reshape(36, 128, 2048))
    sh["bada"] = f(inp["b_ada"][0]).reshape(1, 9216)
    sh["gpm"] = np.ascontiguousarray(np.stack([pm(inp["g_ff1"][0]), pm(inp["g_mix"][0]), pm(inp["g_ff2"][0])], axis=1).reshape(128, 24))
    sh["gfin"] = f(inp["g_final"]).reshape(1, 1024)
    sh["lamv"] = np.concatenate([f(inp[k][0]) for k in ("lambda_q1", "lambda_k1", "lambda_q2", "lambda_k2")]).reshape(1, 256)
    sh["gsub"] = (f(inp["g_subln"][0])).reshape(1, 128)
    cw = f(inp["conv_w"][0])
    sh["convw"] = np.ascontiguousarray(cw.reshape(4, 8, 128).transpose(2, 1, 0).reshape(128, 32))
    sh["cvec"] = np.ascontiguousarray(np.stack([pm(inp["conv_b"][0]), pm(inp["ml_skip"][0]), pm(inp["g_mlnorm"][0])], axis=1).reshape(128, 24))
    wmq = np.stack([f(inp["w_mq"][0]), f(inp["w_mk"][0]), f(inp["w_mv"][0])], axis=0)
    sh["wm"] = np.ascontiguousarray(wmq.reshape(3, 4, 2, 128, 256).transpose(3, 0, 1, 2, 4).reshape(128, 3 * 4 * 2 * 256))
    wif = f(inp["w_if"][0])
    sh["wif"] = np.ascontiguousarray(wif.reshape(3, 4, 2, 128, 8).transpose(3, 0, 1, 2, 4).reshape(128, 3 * 4 * 2 * 8))
    bif = f(inp["b_if"][0])
    sh["bif"] = np.ascontiguousarray(np.stack([bif[0:4], bif[4:8]], axis=1))
    inv = (np.float32(10000.0) ** (-(np.arange(0, 64, 2, dtype=np.float32)) / np.float32(64))).astype(np.float32)
    ang = (np.arange(S, dtype=np.float32)[:, None] * inv[None, :]).astype(np.float32)
    sh["cos"] = np.cos(ang).astype(np.float32)
    sh["sin"] = np.sin(ang).astype(np.float32)
    cst = np.zeros((128, 704), np.float32)
    cst[:, 0:128] = np.eye(128, dtype=np.float32)
    cst[0:64, 128:192] = np.triu(np.ones((64, 64), np.float32))
    for hd in range(4):
        cst[hd, 192 + hd * 128:192 + (hd + 1) * 128] = 1.0
    sh["cst"] = cst
    return sh


def prep_core(inp, b, S):
    f = lambda a: np.ascontiguousarray(np.asarray(a, dtype=np.float32))
    return {"x": f(inp["x"][b, :S]), "cpm": np.ascontiguousarray(f(inp["c"][b]).reshape(8, 128).T)}


_CACHE = {}


def kernel(**inputs):
    NT = 8
    S = NT * 512
    if "nc" not in _CACHE:
        _CACHE["nc"] = build(NT)[0]
    nc = _CACHE["nc"]
    sh = prep_shared(inputs, S)
    in_maps = []
    for b in range(8):
        m = dict(sh)
        m.update(prep_core(inputs, b, S))
        in_maps.append(m)
    res = run_bass_kernel_spmd(nc, in_maps, core_ids=list(range(8)))
    out = np.stack([np.asarray(res.results[b]["out"], dtype=np.float32) for b in range(8)], axis=0)
    return out
```
